# Optimizing a Trainium2 kernel written in Bass

```python
import jax, jax.numpy as jnp
from jax import lax
import numpy as np

D_MODEL = 1024
BATCH = 8
SEQ = 2048
DEPTH = 4

HEAD_DIM = 64
N_HEADS = D_MODEL // HEAD_DIM
CONV_HEADS = N_HEADS // 4
SG_HEADS = N_HEADS // 4
SB_HEADS = N_HEADS // 2
CONV_WIDTH = CONV_HEADS * HEAD_DIM
SG_WIDTH = SG_HEADS * HEAD_DIM
SB_WIDTH = SB_HEADS * HEAD_DIM
MIX_WIDTH = CONV_WIDTH + SG_WIDTH + SB_WIDTH
CONV_KERNEL = 31
SG_CHUNK = 128
SB_BLOCK = 128
OFF_CONV = 0
OFF_SG = OFF_CONV + 2 * CONV_WIDTH
OFF_SB = OFF_SG + 2 * SG_WIDTH
IN_WIDTH = OFF_SB + 3 * SB_WIDTH
FFN_HIDDEN = ((8 * D_MODEL + 3 * 256 - 1) // (3 * 256)) * 256
RMS_EPS = 1e-6
LN_EPS = 1e-5

kernel_name = "hybrid_conv_gmlp_stickbreaking_trunk"


def rms_norm(x, g):
    xf = x.astype(jnp.float32)
    y = xf * lax.rsqrt(jnp.mean(xf * xf, axis=-1, keepdims=True) + RMS_EPS)
    return (y * g.astype(jnp.float32)).astype(x.dtype)


def layer_norm(x, g, b):
    xf = x.astype(jnp.float32)
    mu = jnp.mean(xf, axis=-1, keepdims=True)
    xc = xf - mu
    var = jnp.mean(xc * xc, axis=-1, keepdims=True)
    y = xc * lax.rsqrt(var + LN_EPS) * g.astype(jnp.float32) + b.astype(jnp.float32)
    return y.astype(x.dtype)


def conv_module(val, gate, conv_w, conv_b, ln_g, ln_b):
    h = val * jax.nn.sigmoid(gate)
    h = lax.conv_general_dilated(
        h, conv_w[:, None, :].astype(h.dtype), window_strides=(1,),
        padding=[(CONV_KERNEL - 1, 0)],
        dimension_numbers=('NWC', 'WIO', 'NWC'),
        feature_group_count=CONV_WIDTH) + conv_b
    h = layer_norm(h, ln_g, ln_b)
    return jax.nn.silu(h)


def spatial_gating(uv, ln_g, ln_b, sg_w, sg_b):
    uv = jax.nn.gelu(uv, approximate=False)
    u, v = uv[..., :SG_WIDTH], uv[..., SG_WIDTH:]
    v = layer_norm(v, ln_g, ln_b)
    bsz, seq, _ = v.shape
    n_chunks = seq // SG_CHUNK
    v = v.reshape(bsz, n_chunks, SG_CHUNK, SG_HEADS, HEAD_DIM)
    causal = jnp.tril(jnp.ones((SG_CHUNK, SG_CHUNK), dtype=bool))
    w = jnp.where(causal[None], sg_w, 0)
    mixed = jnp.einsum('hts,bnshd->bnthd', w, v) + sg_b.T[None, None, :, :, None]
    return u * mixed.reshape(bsz, seq, SG_WIDTH)


def stick_breaking_attention(q, k, v):
    scale = HEAD_DIM ** -0.5
    seq = q.shape[2]
    outs = []
    for blk in range(seq // SB_BLOCK):
        t0 = blk * SB_BLOCK
        kv_len = t0 + SB_BLOCK
        qb = q[:, :, t0:kv_len].astype(jnp.float32)
        kb = k[:, :, :kv_len].astype(jnp.float32)
        vb = v[:, :, :kv_len].astype(jnp.float32)
        z = jnp.einsum('bhtd,bhsd->bhts', qb, kb) * scale
        t_pos = t0 + jnp.arange(SB_BLOCK)
        s_pos = jnp.arange(kv_len)
        causal = s_pos[None, :] < t_pos[:, None]
        log_not_beta = jnp.where(causal, jax.nn.log_sigmoid(-z), 0.0)
        between = lax.cumsum(log_not_beta, axis=3, reverse=True) - log_not_beta
        att = jnp.where(causal, jnp.exp(jax.nn.log_sigmoid(z) + between), 0.0)
        outs.append(jnp.einsum('bhts,bhsd->bhtd', att, vb))
    return jnp.concatenate(outs, axis=2).astype(v.dtype)


def hybrid_layer(x, mix_norm_g, w_in, conv_w, conv_b, conv_ln_g, conv_ln_b,
                 sg_ln_g, sg_ln_b, sg_w, sg_b, q_norm_g, k_norm_g, out_norm_g,
                 w_out, ffn_norm_g, w_gate_up, w_down):
    bsz, seq, _ = x.shape
    h = rms_norm(x, mix_norm_g)
    proj = jnp.einsum('bsd,de->bse', h, w_in)

    y_conv = conv_module(proj[..., OFF_CONV:OFF_CONV + CONV_WIDTH],
                         proj[..., OFF_CONV + CONV_WIDTH:OFF_SG],
                         conv_w, conv_b, conv_ln_g, conv_ln_b)

    y_sg = spatial_gating(proj[..., OFF_SG:OFF_SB], sg_ln_g, sg_ln_b, sg_w, sg_b)

    qkv = proj[..., OFF_SB:].reshape(bsz, seq, 3, SB_HEADS, HEAD_DIM)
    q = rms_norm(qkv[:, :, 0], q_norm_g).transpose(0, 2, 1, 3)
    k = rms_norm(qkv[:, :, 1], k_norm_g).transpose(0, 2, 1, 3)
    v = qkv[:, :, 2].transpose(0, 2, 1, 3)
    y_sb = stick_breaking_attention(q, k, v).transpose(0, 2, 1, 3).reshape(bsz, seq, SB_WIDTH)

    y = jnp.concatenate([
        rms_norm(y_conv, out_norm_g[:CONV_WIDTH]),
        rms_norm(y_sg, out_norm_g[CONV_WIDTH:CONV_WIDTH + SG_WIDTH]),
        rms_norm(y_sb, out_norm_g[CONV_WIDTH + SG_WIDTH:]),
    ], axis=-1)
    x = x + jnp.einsum('bse,ed->bsd', y, w_out)

    h = rms_norm(x, ffn_norm_g)
    gu = jnp.einsum('bsd,df->bsf', h, w_gate_up)
    act = jax.nn.silu(gu[..., :FFN_HIDDEN]) * gu[..., FFN_HIDDEN:]
    return x + jnp.einsum('bsf,fd->bsd', act, w_down)


def setup_inputs(seed: int = 0) -> dict:
    key = jax.random.key(seed)
    ks = jax.random.split(key, 20)
    f32 = jnp.float32
    nrm = lambda k, shape: jax.random.normal(k, shape, dtype=f32)
    L = DEPTH
    return {
        "x": nrm(ks[0], (BATCH, SEQ, D_MODEL)),
        "mix_norm_g": 1.0 + 0.02 * nrm(ks[1], (L, D_MODEL)),
        "w_in": nrm(ks[2], (L, D_MODEL, IN_WIDTH)) * D_MODEL ** -0.5,
        "conv_w": nrm(ks[3], (L, CONV_KERNEL, CONV_WIDTH)) * CONV_KERNEL ** -0.5,
        "conv_b": 0.02 * nrm(ks[4], (L, CONV_WIDTH)),
        "conv_ln_g": 1.0 + 0.02 * nrm(ks[5], (L, CONV_WIDTH)),
        "conv_ln_b": 0.02 * nrm(ks[6], (L, CONV_WIDTH)),
        "sg_ln_g": 1.0 + 0.02 * nrm(ks[7], (L, SG_WIDTH)),
        "sg_ln_b": 0.02 * nrm(ks[8], (L, SG_WIDTH)),
        "sg_w": nrm(ks[9], (L, SG_HEADS, SG_CHUNK, SG_CHUNK)) * SG_CHUNK ** -0.5,
        "sg_b": 1.0 + 0.1 * nrm(ks[10], (L, SG_HEADS, SG_CHUNK)),
        "q_norm_g": 1.0 + 0.02 * nrm(ks[11], (L, HEAD_DIM)),
        "k_norm_g": 1.0 + 0.02 * nrm(ks[12], (L, HEAD_DIM)),
        "out_norm_g": 1.0 + 0.02 * nrm(ks[13], (L, MIX_WIDTH)),
        "w_out": nrm(ks[14], (L, MIX_WIDTH, D_MODEL)) * MIX_WIDTH ** -0.5,
        "ffn_norm_g": 1.0 + 0.02 * nrm(ks[15], (L, D_MODEL)),
        "w_gate_up": nrm(ks[16], (L, D_MODEL, 2 * FFN_HIDDEN)) * D_MODEL ** -0.5,
        "w_down": nrm(ks[17], (L, FFN_HIDDEN, D_MODEL)) * FFN_HIDDEN ** -0.5,
    }


def reference(x, mix_norm_g, w_in, conv_w, conv_b, conv_ln_g, conv_ln_b,
              sg_ln_g, sg_ln_b, sg_w, sg_b, q_norm_g, k_norm_g, out_norm_g,
              w_out, ffn_norm_g, w_gate_up, w_down):
    for l in range(DEPTH):
        x = hybrid_layer(x, mix_norm_g[l], w_in[l], conv_w[l], conv_b[l],
                         conv_ln_g[l], conv_ln_b[l], sg_ln_g[l], sg_ln_b[l],
                         sg_w[l], sg_b[l], q_norm_g[l], k_norm_g[l], out_norm_g[l],
                         w_out[l], ffn_norm_g[l], w_gate_up[l], w_down[l])
    return x
```

```python
import numpy as np
import ml_dtypes
import concourse.bass as bass
import concourse.mybir as mybir
from concourse.bass_utils import run_bass_kernel_spmd

F32 = mybir.dt.float32
BF16 = mybir.dt.bfloat16
AF = mybir.ActivationFunctionType
ALU = mybir.AluOpType

S = 2048
D = 1024
L_FULL = 4
NT = 4
TT = 512
HID = 2816
NSLOT = 5
MAX_INFLIGHT = 1
WSZ = 2048
RMS_EPS = 1e-6
LN_EPS = 1e-5
WTOT = 36 * 2048 + 16 * 1408
NPCOL = 94
PMISC = 2048

SAME_ENGINE_SYNC = True


class Chan:
    def __init__(self, sem, name):
        self.sem = sem
        self.count = 0
        self.name = name


class Buf:
    __slots__ = ("name", "w", "readers", "excl")

    def __init__(self, name, excl=False):
        self.name = name
        self.w = None
        self.readers = {}
        self.excl = excl


class Eng:
    def __init__(self, name, chan):
        self.name = name
        self.chan = chan
        self.ops = []
        self.waited = {}
        self.pend_r = []
        self.pend_w = []

    def _waits(self, reads, writes):
        need = {}

        def req(tag):
            ch, val = tag
            if ch is self.chan and (self.name == "pe" or not SAME_ENGINE_SYNC):
                return
            if need.get(ch, 0) < val:
                need[ch] = val

        for b in reads:
            if b.w is not None:
                req(b.w)
        for b in writes:
            if b.w is not None:
                req(b.w)
            for ch, val in b.readers.items():
                req((ch, val))
        for ch, val in need.items():
            if self.waited.get(ch, 0) < val:
                self.waited[ch] = val
                self.ops.append(lambda e, s=ch.sem, v=val: e.wait_ge(s, v))

    def op(self, fn, reads=(), writes=(), inc=True):
        if any(b.excl for b in reads):
            writes = list(writes) + [b for b in reads if b.excl]
            reads = [b for b in reads if not b.excl]
        self._waits(reads, writes)
        if not inc:
            self.pend_r.extend(reads)
            self.pend_w.extend(writes)
            self.ops.append(lambda e: fn(e))
            return
        self.chan.count += 1
        tag = (self.chan, self.chan.count)
        self.ops.append(lambda e, s=self.chan.sem: fn(e).then_inc(s, 1))
        for b in list(reads) + self.pend_r:
            if b.readers.get(tag[0], 0) < tag[1]:
                b.readers[tag[0]] = tag[1]
        for b in list(writes) + self.pend_w:
            b.w = tag
            b.readers = {}
        self.pend_r = []
        self.pend_w = []

    def dma(self, fn, chan, reads=(), writes=()):
        self._waits(reads, writes)
        chan.count += 16
        tag = (chan, chan.count)
        self.ops.append(lambda e, s=chan.sem: fn(e).then_inc(s, 16))
        for b in reads:
            if b.readers.get(chan, 0) < tag[1]:
                b.readers[chan] = tag[1]
        for b in writes:
            b.w = tag
            b.readers = {}

    def wait_chan(self, chan, val):
        if self.waited.get(chan, 0) < val:
            self.waited[chan] = val
            self.ops.append(lambda e, s=chan.sem, v=val: e.wait_ge(s, v))


class _Stop(Exception):
    pass


def build_program(NL, stop_after=None):
    nc = bass.Bass("TRN2", target_bir_lowering=False)

    d_xT = nc.dram_tensor("xT", [128, 8, S], F32, kind="ExternalInput").ap()
    d_wl = [nc.dram_tensor(f"wflat{l}", [128, WTOT], F32, kind="ExternalInput").ap() for l in range(NL)]
    d_pcol = nc.dram_tensor("pcol", [128, NL * NPCOL], F32, kind="ExternalInput").ap()
    d_pmisc = nc.dram_tensor("pmisc", [NL, 128, PMISC], F32, kind="ExternalInput").ap()
    d_cst = nc.dram_tensor("cst", [128, 8, 128], BF16, kind="ExternalInput").ap()
    d_out = nc.dram_tensor("outT", [128, 8, S], F32, kind="ExternalOutput").ap()

    def mkchan(name):
        return Chan(nc.alloc_semaphore(name), name)

    PE = Eng("pe", mkchan("c_pe"))
    ACT = Eng("act", mkchan("c_act"))
    DVE = Eng("dve", mkchan("c_dve"))
    POOL = Eng("pool", mkchan("c_pool"))
    SP = Eng("sp", mkchan("c_sp"))
    engines = [PE, ACT, DVE, POOL, SP]

    def barrier():
        for e in (PE, ACT, DVE):
            for f in (PE, ACT, DVE):
                if f is not e and f.chan.count > 0:
                    e.wait_chan(f.chan, f.chan.count)

    xT = nc.alloc_sbuf_tensor("xT_sb", [128, 8, S], F32)
    hT = nc.alloc_sbuf_tensor("hT_sb", [128, 8, S], BF16)
    r1 = nc.alloc_sbuf_tensor("r1_sb", [128, 11, S], BF16)
    wsl = nc.alloc_sbuf_tensor("wslots", [128, NSLOT, WSZ], BF16)
    wk32 = nc.alloc_sbuf_tensor("wk32", [128, 4, TT], F32)
    wk16 = nc.alloc_sbuf_tensor("wk16", [128, 8, TT], BF16)
    PHW = 2 * (30 + S)
    ph = nc.alloc_sbuf_tensor("phase", [128, PHW], F32)
    cst = nc.alloc_sbuf_tensor("cst_sb", [128, 8, 128], BF16)
    pcol = nc.alloc_sbuf_tensor("pcol_sb", [128, NL * NPCOL], F32)
    pmisc = nc.alloc_sbuf_tensor("pmisc_sb", [128, PMISC], F32)
    sgw16 = nc.alloc_sbuf_tensor("sgw16", [128, 4, 128], BF16)
    small = nc.alloc_sbuf_tensor("small", [128, 64], F32)
    qgs = nc.alloc_sbuf_tensor("qgs", [128, NL], F32)
    cacc = nc.alloc_sbuf_tensor("cacc", [128, TT], F32)

    hc = ph[:, :].rearrange("p (c t) -> p c t", c=2)
    SS = pmisc[:, 0:S]

    psall = nc.alloc_psum_tensor("psall", [128, 8, TT], F32)
    psum = [psall[:, i, :] for i in range(8)]

    b_xT = [[Buf(f"xT{c}_{t}") for t in range(NT)] for c in range(8)]
    b_hT = [[Buf(f"hT{c}_{t}") for t in range(NT)] for c in range(8)]
    b_r1 = [[Buf(f"r1{c}_{t}") for t in range(NT)] for c in range(11)]
    b_ws = [Buf(f"ws{i}") for i in range(NSLOT)]
    b_w32 = [Buf(f"wk32_{i}") for i in range(4)]
    b_w16 = [Buf(f"wk16_{i}") for i in range(8)]
    b_ps = [Buf(f"ps{i}", excl=True) for i in range(8)]
    b_hc = [[Buf(f"hc{c}_{t}") for t in range(NT)] for c in range(2)]
    b_hcpad = Buf("hcpad")
    b_SS = [Buf(f"SS{t}") for t in range(NT)]
    b_cst = Buf("cst")
    b_pcol = Buf("pcol")
    b_pmisc = Buf("pmisc")
    b_sgw16 = Buf("sgw16")
    b_small = Buf("small")
    b_qgs = Buf("qgs")
    b_cacc = Buf("cacc")

    ch_ws = [mkchan(f"c_ws{i}") for i in range(NSLOT)]
    ch_x = mkchan("c_x")
    ch_c = mkchan("c_cst")
    ch_m = mkchan("c_misc")
    ch_o = mkchan("c_out")

    def tsl(t):
        return slice(t * TT, (t + 1) * TT)

    W32 = [wk32[:, i, :] for i in range(4)]
    W16 = [wk16[:, i, :] for i in range(8)]
    ONES = cst[:, 0, :]
    BD64 = cst[:, 1, :]
    NEGU = cst[:, 2, :]
    NEGONES = cst[:, 3, :]
    MASK_LT = cst[:, 4, :]
    MASK_LE = cst[:, 5, :]
    IDENT = cst[:, 6, :]
    NEGMASK = cst[:, 7, :]

    ps_rr = [0]

    def next_ps():
        i = ps_rr[0]
        ps_rr[0] = (i + 1) % 6
        return i

    def next_ps_pair():
        i = ps_rr[0]
        if i % 2:
            i = (i + 1) % 6
        ps_rr[0] = (i + 2) % 6
        return i

    def mm(out, lhsT, rhs, start, stop, reads, writes, inc, skip=False):
        PE.op(lambda e: e.matmul(out, lhsT=lhsT, rhs=rhs, start=start, stop=stop, skip_group_check=skip),
              reads=reads, writes=writes, inc=inc)

    def act(out, in_, func, reads, writes, bias=None, scale=None):
        kw = {}
        if bias is not None:
            kw["bias"] = bias
        if scale is not None:
            kw["scale"] = scale
        ACT.op(lambda e: e.activation(out=out, in_=in_, func=func, **kw), reads=reads, writes=writes)

    def v_tt(out, in0, in1, op, reads, writes):
        DVE.op(lambda e: e.tensor_tensor(out=out, in0=in0, in1=in1, op=op), reads=reads, writes=writes)

    def v_ts(out, in0, s1, s2, op0, op1, reads, writes):
        if s2 is None:
            DVE.op(lambda e: e.tensor_scalar(out=out, in0=in0, scalar1=s1, scalar2=None, op0=op0),
                   reads=reads, writes=writes)
        else:
            DVE.op(lambda e: e.tensor_scalar(out=out, in0=in0, scalar1=s1, scalar2=s2, op0=op0, op1=op1),
                   reads=reads, writes=writes)

    def v_stt(out, in0, scalar, in1, op0, op1, reads, writes):
        DVE.op(lambda e: e.scalar_tensor_tensor(out=out, in0=in0, scalar=scalar, in1=in1, op0=op0, op1=op1),
               reads=reads, writes=writes)

    def v_copy(out, in_, reads, writes):
        DVE.op(lambda e: e.tensor_copy(out=out, in_=in_), reads=reads, writes=writes)

    def rstd_from(ps_ap, scale, eps, tmp_i, out_i, rd_bufs):
        act(W32[tmp_i], ps_ap, AF.Ln, reads=rd_bufs, writes=[b_w32[tmp_i]], bias=eps, scale=scale)
        act(W32[out_i], W32[tmp_i], AF.Exp, reads=[b_w32[tmp_i]], writes=[b_w32[out_i]], scale=-0.5)

    pieces = []
    for l in range(NL):
        off = 0
        for _ in range(14):
            pieces.append((l, off, 2048)); off += 2048
        for g in range(2):
            for _ in range(11):
                pieces.append((l, off, 2048)); off += 2048
            for _ in range(8):
                pieces.append((l, off, 1408)); off += 1408
        assert off == WTOT
    wstate = {"issued": 0, "got": 0, "free": list(range(NSLOT)), "slot_of": {}}

    def w_issue():
        while wstate["free"] and wstate["issued"] < len(pieces):
            i = wstate["issued"]
            sl = wstate["free"].pop(0)
            l, off, sz = pieces[i]
            hist = wstate.setdefault("hist", [])
            if len(hist) >= MAX_INFLIGHT:
                pch, pval = hist[-MAX_INFLIGHT]
                POOL.wait_chan(pch, pval)
            hist.append((ch_ws[sl], ch_ws[sl].count + 16))
            POOL.dma(lambda e, sl=sl, l=l, off=off, sz=sz: e.dma_start(out=wsl[:, sl, 0:sz], in_=d_wl[l][:, off:off + sz]),
                     ch_ws[sl], reads=(), writes=[b_ws[sl]])
            wstate["slot_of"][i] = sl
            wstate["issued"] += 1

    def w_get():
        i = wstate["got"]
        assert i in wstate["slot_of"], "weight piece not issued (too many live pieces)"
        wstate["got"] += 1
        return i, wstate["slot_of"][i]

    def w_release(i):
        sl = wstate["slot_of"].pop(i)
        wstate["free"].append(sl)
        w_issue()

    SP.dma(lambda e: e.dma_start(out=cst[:], in_=d_cst), ch_c, writes=[b_cst])
    SP.dma(lambda e: e.dma_start(out=pcol[:], in_=d_pcol), ch_c, writes=[b_pcol])
    for c in range(8):
        SP.dma(lambda e, c=c: e.dma_start(out=xT[:, c, :], in_=d_xT[:, c, :]), ch_x,
               writes=[b_xT[c][t] for t in range(NT)])
    for c in range(8):
        for t in range(NT):
            b_xT[c][t].w = (ch_x, ch_x.count)
    b_cst.w = (ch_c, ch_c.count)
    b_pcol.w = (ch_c, ch_c.count)
    w_issue()

    for l in range(NL):
        v_ts(qgs[:, l:l + 1], pcol[:, l * NPCOL + 92:l * NPCOL + 93], 0.125, None, ALU.mult, None,
             reads=[b_pcol], writes=[b_qgs])

    def pc(l, j):
        return pcol[:, l * NPCOL + j:l * NPCOL + j + 1]

    def rmsnorm_to_hT(l, goff):
        for t in range(NT):
            bank = next_ps()
            for c in range(8):
                si = c % 4
                act(W16[si], xT[:, c, tsl(t)], AF.Square, reads=[b_xT[c][t]], writes=[b_w16[si]])
                mm(psum[bank][:, :], ONES, W16[si], c == 0, c == 7,
                   reads=[b_w16[si], b_cst], writes=[b_ps[bank]], inc=True)
            rstd_from(psum[bank][:, :], 1.0 / D, RMS_EPS, 0, 1, [b_ps[bank]])
            for c in range(8):
                v_stt(hT[:, c, tsl(t)], xT[:, c, tsl(t)], pc(l, goff + c), W32[1], ALU.mult, ALU.mult,
                      reads=[b_xT[c][t], b_w32[1], b_pcol], writes=[b_hT[c][t]])

    def proj_fm(bank, slot, col0, t, last_inc=True):
        for kc in range(8):
            mm(psum[bank][:, :], wsl[:, slot, kc * 256 + col0: kc * 256 + col0 + 128], hT[:, kc, tsl(t)],
               kc == 0, kc == 7, reads=[b_ws[slot], b_hT[kc][t]], writes=[b_ps[bank]],
               inc=(kc == 7))

    def phase(k):
        if stop_after is not None and k >= stop_after:
            raise _Stop()

    for l in range(NL):
      try:
        SP.dma(lambda e, l=l: e.dma_start(out=pmisc[:], in_=d_pmisc[l]), ch_m, writes=[b_pmisc] + b_SS)
        for h in range(4):
            v_tt(sgw16[:, h, :], pmisc[:, 1536 + h * 128:1536 + (h + 1) * 128], MASK_LE, ALU.mult,
                 reads=[b_pmisc, b_cst], writes=[b_sgw16])
        DVE.op(lambda e: e.memset(hc[:, :, 0:30], 0.0), writes=[b_hcpad])

        phase(0)
        rmsnorm_to_hT(l, 0)
        phase(1)

        pv, sv = w_get()
        pg, sg_ = w_get()
        for fc in range(2):
            for t in range(NT):
                bv = next_ps()
                proj_fm(bv, sv, fc * 128, t)
                bg = next_ps()
                proj_fm(bg, sg_, fc * 128, t)
                act(W32[0], psum[bg][:, :], AF.Sigmoid, reads=[b_ps[bg]], writes=[b_w32[0]])
                v_tt(hc[:, fc, 30 + t * TT:30 + (t + 1) * TT], psum[bv][:, :], W32[0], ALU.mult,
                     reads=[b_ps[bv], b_w32[0]], writes=[b_hc[fc][t]])
        w_release(pv)
        w_release(pg)
        phase(2)
        fillers = []
        for fc in range(2):
            for t in range(NT - 1, -1, -1):
                rd = [b_hc[fc][t], b_hcpad, b_pcol] + ([b_hc[fc][t - 1]] if t > 0 else [])
                cw0 = 24 + fc * 31
                fillers.append(lambda fc=fc, t=t, rd=rd, cw0=cw0: v_ts(
                    cacc[:, :], hc[:, fc, t * TT:t * TT + TT], pc(l, cw0), pc(l, 86 + fc), ALU.mult, ALU.add,
                    reads=rd, writes=[b_cacc]))
                for k in range(1, 30):
                    fillers.append(lambda fc=fc, t=t, rd=rd, cw0=cw0, k=k: v_stt(
                        cacc[:, :], hc[:, fc, t * TT + k:t * TT + k + TT], pc(l, cw0 + k), cacc[:, :], ALU.mult, ALU.add,
                        reads=rd + [b_cacc], writes=[b_cacc]))
                fillers.append(lambda fc=fc, t=t, rd=rd, cw0=cw0: v_stt(
                    hc[:, fc, 30 + t * TT:30 + t * TT + TT], hc[:, fc, 30 + t * TT:30 + t * TT + TT],
                    pc(l, cw0 + 30), cacc[:, :], ALU.mult, ALU.add,
                    reads=rd + [b_cacc], writes=[b_hc[fc][t]]))

        def run_fillers(n):
            for _ in range(n):
                if fillers:
                    fillers.pop(0)()

        phase(4)
        pvv, svv = w_get()
        pu, su = w_get()
        for t in range(NT):
            for sub in range(4):
                ck = 4 * t + sub
                bank = next_ps()
                for kc in range(8):
                    mm(psum[bank][:, 0:256], hT[:, kc, ck * 128:(ck + 1) * 128], wsl[:, svv, kc * 256:(kc + 1) * 256],
                       kc == 0, kc == 7, reads=[b_ws[svv], b_hT[kc][t]], writes=[b_ps[bank]], inc=(kc == 7))
                wi = sub // 2
                vg = wk32[:, wi, (sub % 2) * 256:(sub % 2) * 256 + 256]
                act(vg, psum[bank][:, 0:256], AF.Gelu, reads=[b_ps[bank]], writes=[b_w32[wi]])
                DVE.op(lambda e, vg=vg, sub=sub: e.bn_stats(out=small[:, sub * 6:sub * 6 + 6], in_=vg),
                       reads=[b_w32[wi]], writes=[b_small])
                DVE.op(lambda e, sub=sub: e.bn_aggr(out=small[:, 24 + sub * 2:24 + sub * 2 + 2],
                                                    in_=small[:, sub * 6:sub * 6 + 6]),
                       reads=[b_small], writes=[b_small])
            var4 = small[:, 24:32].rearrange("p (a b) -> p a b", b=2)[:, :, 1]
            act(small[:, 32:36], var4, AF.Ln, reads=[b_small], writes=[b_small], bias=LN_EPS)
            act(small[:, 36:40], small[:, 32:36], AF.Exp, reads=[b_small], writes=[b_small], scale=-0.5)
            for sub in range(4):
                wi = sub // 2
                vg = wk32[:, wi, (sub % 2) * 256:(sub % 2) * 256 + 256]
                v_ts(vg, vg, small[:, 24 + sub * 2:25 + sub * 2], small[:, 36 + sub:37 + sub], ALU.subtract, ALU.mult,
                     reads=[b_w32[wi], b_small], writes=[b_w32[wi]])
                v_tt(vg, vg, pmisc[:, 0:256], ALU.mult, reads=[b_w32[wi], b_pmisc], writes=[b_w32[wi]])
                vh = wk16[:, wi, (sub % 2) * 256:(sub % 2) * 256 + 256]
                v_tt(vh, vg, pmisc[:, 256:512], ALU.add, reads=[b_w32[wi], b_pmisc], writes=[b_w16[wi]])
            for fc in range(2):
                bank = next_ps()
                proj_fm(bank, su, fc * 128, t)
                act(W32[fc], psum[bank][:, :], AF.Gelu, reads=[b_ps[bank]], writes=[b_w32[fc]])
            for fc in range(2):
                bank = next_ps()
                for sub in range(4):
                    for hh in range(2):
                        h = 2 * fc + hh
                        wi = sub // 2
                        vh = wk16[:, wi, (sub % 2) * 256 + h * 64:(sub % 2) * 256 + h * 64 + 64]
                        mm(psum[bank][hh * 64:(hh + 1) * 64, sub * 128:(sub + 1) * 128], vh, sgw16[:, h, :],
                           True, True, reads=[b_w16[wi], b_sgw16], writes=[b_ps[bank]],
                           inc=(sub == 3 and hh == 1))
                v_tt(W32[2 + fc], psum[bank][:, :], pmisc[:, 512 + fc * 512:512 + (fc + 1) * 512], ALU.add,
                     reads=[b_ps[bank], b_pmisc], writes=[b_w32[2 + fc]])
                v_tt(W32[2 + fc], W32[2 + fc], W32[fc], ALU.mult, reads=[b_w32[2 + fc], b_w32[fc]],
                     writes=[b_w32[2 + fc]])
                act(W16[2 + fc], W32[2 + fc], AF.Square, reads=[b_w32[2 + fc]], writes=[b_w16[2 + fc]])
            b3 = next_ps()
            for fc in range(2):
                mm(psum[b3][:, :], ONES, W16[2 + fc], fc == 0, fc == 1, reads=[b_w16[2 + fc], b_cst],
                   writes=[b_ps[b3]], inc=True)
            rstd_from(psum[b3][:, :], 1.0 / 256, RMS_EPS, 0, 1, [b_ps[b3]])
            for fc in range(2):
                v_stt(r1[:, 2 + fc, tsl(t)], W32[2 + fc], pc(l, 18 + fc), W32[1], ALU.mult, ALU.mult,
                      reads=[b_w32[2 + fc], b_w32[1], b_pcol], writes=[b_r1[2 + fc][t]])
        w_release(pvv)
        w_release(pu)

        phase(5)
        qT = r1[:, 8, :]
        kT = r1[:, 9, :]
        v16 = r1[:, 10, :].rearrange("p (c d) -> p c d", d=128)
        pieces_v = {}
        for hp in range(4):
            pqk, sqk = w_get()
            if hp % 2 == 0:
                pieces_v["cur"] = w_get()
            pvp, svp = pieces_v["cur"]
            for t in range(NT):
                for which in range(2):
                    bank = next_ps()
                    proj_fm(bank, sqk, which * 128, t)
                    v_copy(W32[0], psum[bank][:, :], reads=[b_ps[bank]], writes=[b_w32[0]])
                    act(W16[0], psum[bank][:, :], AF.Square, reads=[b_ps[bank]], writes=[b_w16[0]])
                    b2 = next_ps()
                    mm(psum[b2][:, :], BD64, W16[0], True, True, reads=[b_w16[0], b_cst], writes=[b_ps[b2]], inc=True)
                    rstd_from(psum[b2][:, :], 1.0 / 64, RMS_EPS, 1, 2, [b_ps[b2]])
                    gcol = qgs[:, l:l + 1] if which == 0 else pc(l, 93)
                    dst = (qT if which == 0 else kT)[:, tsl(t)]
                    v_stt(dst, W32[0], gcol, W32[2], ALU.mult, ALU.mult,
                          reads=[b_w32[0], b_w32[2], b_qgs, b_pcol], writes=[b_r1[8 + which][t]])
            phase(5.2)
            for t in range(NT):
                bank = next_ps()
                for sub in range(4):
                    ck = 4 * t + sub
                    for kc in range(8):
                        c0 = kc * 256 + (hp % 2) * 128
                        mm(psum[bank][:, sub * 128:(sub + 1) * 128], hT[:, kc, ck * 128:(ck + 1) * 128],
                           wsl[:, svp, c0:c0 + 128], kc == 0, kc == 7,
                           reads=[b_ws[svp], b_hT[kc][t]], writes=[b_ps[bank]], inc=(kc == 7 and sub == 3))
                v_copy(r1[:, 10, tsl(t)], psum[bank][:, :], reads=[b_ps[bank]], writes=[b_r1[10][t]])
            w_release(pqk)
            if hp % 2 == 1:
                w_release(pvp)
            phase(5.4)
            iters = [(qt, kb) for qt in range(NT) for kb in range(4 * qt + 3, -1, -1)]
            ctxs = {}

            def setup(i):
                qt, kb = iters[i]
                ob = 6 + (qt % 2)
                diag = kb >= 4 * qt
                c0 = (kb - 4 * qt) * 128 if diag else 0
                kt = kb // 4
                cs = slice(c0, TT)
                first = (kb == 4 * qt + 3)
                cc0 = c0 + 128 if diag else 0
                has_carry = (not first) and cc0 < TT
                par = kb % 2
                zb = next_ps_pair()
                ctxs[i] = (qt, kb, ob, diag, c0, kt, cs, first, cc0, has_carry, par, zb)

            def stage_a1(i):
                qt, kb, ob, diag, c0, kt, cs, first, cc0, has_carry, par, zb = ctxs[i]
                bz = [b_ps[zb], b_ps[zb + 1]]
                bE = [b_w32[0], b_w32[1]]
                for hh in range(2):
                    hs = slice(hh * 64, hh * 64 + 64)
                    mm(psall[:, zb + hh, cs], kT[hs, kb * 128:(kb + 1) * 128], qT[hs, qt * TT + c0:(qt + 1) * TT],
                       True, True, reads=[b_r1[9][kt], b_r1[8][qt]], writes=[b_ps[zb + hh]], inc=not diag)
                    if diag:
                        mm(psall[:, zb + hh, c0:c0 + 128], IDENT, NEGMASK, False, True,
                           reads=[b_cst], writes=[b_ps[zb + hh]], inc=True, skip=True)
                act(wk32[:, 0:2, cs], psall[:, zb:zb + 2, cs], AF.Exp, reads=bz, writes=bE)

            def stage_a2(i):
                qt, kb, ob, diag, c0, kt, cs, first, cc0, has_carry, par, zb = ctxs[i]
                L0 = 2 * par
                bE = [b_w32[0], b_w32[1]]
                bL = [b_w16[L0], b_w16[L0 + 1]]
                act(wk16[:, L0:L0 + 2, cs], wk32[:, 0:2, cs], AF.Ln, reads=bE, writes=bL, bias=1.0)

            def stage_b1(i):
                qt, kb, ob, diag, c0, kt, cs, first, cc0, has_carry, par, zb = ctxs[i]
                L0 = 2 * par
                bz = [b_ps[zb], b_ps[zb + 1]]
                bL = [b_w16[L0], b_w16[L0 + 1]]
                bA = [b_w16[4], b_w16[5]]
                bR = [b_w32[2], b_w32[3]]
                bR16 = [b_w16[6], b_w16[7]]
                for hh in range(2):
                    mm(psall[:, zb + hh, cs], NEGU, wk16[:, L0 + hh, cs], False, not has_carry,
                       reads=[b_w16[L0 + hh], b_cst], writes=[b_ps[zb + hh]], inc=not has_carry, skip=True)
                    if has_carry:
                        mm(psall[:, zb + hh, cc0:TT], NEGONES, wk16[:, 6 + hh, cc0:TT], False, True,
                           reads=[b_w16[6 + hh], b_cst], writes=[b_ps[zb + hh]], inc=True, skip=True)
                act(wk16[:, 4:6, cs], psall[:, zb:zb + 2, cs], AF.Exp, reads=bz, writes=bA)

            def stage_b2(i):
                qt, kb, ob, diag, c0, kt, cs, first, cc0, has_carry, par, zb = ctxs[i]
                L0 = 2 * par
                bL = [b_w16[L0], b_w16[L0 + 1]]
                bR = [b_w32[2], b_w32[3]]
                bR16 = [b_w16[6], b_w16[7]]
                for hh in range(2):
                    hs = slice(hh * 64, hh * 64 + 64)
                    mm(psum[ob][hs, cs], v16[:, kb, hs], wk16[:, 4 + hh, cs], first, kb == 0,
                       reads=[b_r1[10][kt], b_w16[4 + hh]], writes=[b_ps[ob]], inc=True, skip=True)
                if kb > 0:
                    if diag:
                        v_copy(wk32[:, 2:4, c0:c0 + 128], wk16[:, L0:L0 + 2, c0:c0 + 128], reads=bL, writes=bR)
                        if c0 + 128 < TT:
                            v_tt(wk32[:, 2:4, c0 + 128:TT], wk32[:, 2:4, c0 + 128:TT], wk16[:, L0:L0 + 2, c0 + 128:TT],
                                 ALU.add, reads=bR + bL, writes=bR)
                    else:
                        v_tt(wk32[:, 2:4, :], wk32[:, 2:4, :], wk16[:, L0:L0 + 2, :], ALU.add,
                             reads=bR + bL, writes=bR)
                    v_copy(wk16[:, 6:8, cs], wk32[:, 2:4, cs], reads=bR, writes=bR16)
                if kb == 0:
                    phase(5.5)
                    v_copy(r1[:, 4 + hp, tsl(qt)], psum[ob][:, :], reads=[b_ps[ob]], writes=[b_r1[4 + hp][qt]])
                    act(W16[0], psum[ob][:, :], AF.Square, reads=[b_ps[ob]], writes=[b_w16[0]])
                    b2 = next_ps()
                    mm(psum[b2][:, :], ONES, W16[0], True, True, reads=[b_w16[0], b_cst], writes=[b_ps[b2]], inc=True)
                    if hp == 0:
                        v_copy(SS[:, tsl(qt)], psum[b2][:, :], reads=[b_ps[b2]], writes=[b_SS[qt], b_pmisc])
                    else:
                        v_tt(SS[:, tsl(qt)], SS[:, tsl(qt)], psum[b2][:, :], ALU.add,
                             reads=[b_ps[b2], b_SS[qt]], writes=[b_SS[qt], b_pmisc])

            setup(0)
            stage_a1(0)
            stage_a2(0)
            for i in range(len(iters)):
                nxt = i + 1 < len(iters)
                if nxt:
                    setup(i + 1)
                    stage_a1(i + 1)
                stage_b1(i)
                if nxt:
                    stage_a2(i + 1)
                stage_b2(i)
                run_fillers(2 if i % 2 == 0 else 1)
        run_fillers(len(fillers))
        phase(3)
        for t in range(NT):
            b1 = next_ps()
            b2 = next_ps()
            for fc in range(2):
                cv = hc[:, fc, 30 + t * TT:30 + (t + 1) * TT]
                act(W16[fc], cv, AF.Copy, reads=[b_hc[fc][t]], writes=[b_w16[fc]])
                act(W16[2 + fc], cv, AF.Square, reads=[b_hc[fc][t]], writes=[b_w16[2 + fc]])
            for fc in range(2):
                mm(psum[b1][:, :], ONES, W16[fc], fc == 0, fc == 1, reads=[b_w16[fc], b_cst],
                   writes=[b_ps[b1]], inc=True)
            for fc in range(2):
                mm(psum[b2][:, :], ONES, W16[2 + fc], fc == 0, fc == 1, reads=[b_w16[2 + fc], b_cst],
                   writes=[b_ps[b2]], inc=True)
            v_ts(W32[0], psum[b1][:, :], 1.0 / 256, None, ALU.mult, None, reads=[b_ps[b1]], writes=[b_w32[0]])
            v_tt(W32[1], W32[0], W32[0], ALU.mult, reads=[b_w32[0]], writes=[b_w32[1]])
            v_stt(W32[1], psum[b2][:, :], 1.0 / 256, W32[1], ALU.mult, ALU.subtract,
                  reads=[b_ps[b2], b_w32[1]], writes=[b_w32[1]])
            rstd_from(W32[1], 1.0, LN_EPS, 2, 3, [b_w32[1]])
            for fc in range(2):
                cv = hc[:, fc, 30 + t * TT:30 + (t + 1) * TT]
                v_tt(cv, cv, W32[0], ALU.subtract, reads=[b_hc[fc][t], b_w32[0]], writes=[b_hc[fc][t]])
                v_tt(cv, cv, W32[3], ALU.mult, reads=[b_hc[fc][t], b_w32[3]], writes=[b_hc[fc][t]])
                act(cv, cv, AF.Silu, reads=[b_hc[fc][t], b_pcol], writes=[b_hc[fc][t]],
                    bias=pc(l, 90 + fc), scale=pc(l, 88 + fc))
                act(W16[4 + fc], cv, AF.Square, reads=[b_hc[fc][t]], writes=[b_w16[4 + fc]])
            b3 = next_ps()
            for fc in range(2):
                mm(psum[b3][:, :], ONES, W16[4 + fc], fc == 0, fc == 1, reads=[b_w16[4 + fc], b_cst],
                   writes=[b_ps[b3]], inc=True)
            rstd_from(psum[b3][:, :], 1.0 / 256, RMS_EPS, 0, 1, [b_ps[b3]])
            for fc in range(2):
                cv = hc[:, fc, 30 + t * TT:30 + (t + 1) * TT]
                v_stt(r1[:, fc, tsl(t)], cv, pc(l, 16 + fc), W32[1], ALU.mult, ALU.mult,
                      reads=[b_hc[fc][t], b_w32[1], b_pcol], writes=[b_r1[fc][t]])

        for t in range(NT):
            rstd_from(SS[:, tsl(t)], 1.0 / 512, RMS_EPS, 0, 1, [b_SS[t]])
            for c in range(4):
                v_stt(r1[:, 4 + c, tsl(t)], r1[:, 4 + c, tsl(t)], pc(l, 20 + c), W32[1], ALU.mult, ALU.mult,
                      reads=[b_r1[4 + c][t], b_w32[1], b_pcol], writes=[b_r1[4 + c][t]])

        phase(6)
        for j in range(4):
            pw, sw = w_get()
            for nn in range(2):
                n = 2 * j + nn
                for t in range(NT):
                    bank = next_ps()
                    for idx, kc in enumerate((2, 3, 4, 5, 6, 7, 0, 1)):
                        mm(psum[bank][:, :], wsl[:, sw, kc * 256 + nn * 128:kc * 256 + nn * 128 + 128],
                           r1[:, kc, tsl(t)], idx == 0, idx == 7,
                           reads=[b_ws[sw], b_r1[kc][t]], writes=[b_ps[bank]], inc=(idx == 7))
                    v_tt(xT[:, n, tsl(t)], psum[bank][:, :], xT[:, n, tsl(t)], ALU.add,
                         reads=[b_ps[bank], b_xT[n][t]], writes=[b_xT[n][t]])
            w_release(pw)

        phase(7)
        rmsnorm_to_hT(l, 8)
        for g in range(2):
            for cc in range(11):
                pw, sw = w_get()
                for t in range(NT):
                    bg = next_ps()
                    proj_fm(bg, sw, 0, t)
                    bu = next_ps()
                    proj_fm(bu, sw, 128, t)
                    si = t % 2
                    act(W32[si], psum[bg][:, :], AF.Silu, reads=[b_ps[bg]], writes=[b_w32[si]])
                    v_tt(r1[:, cc, tsl(t)], W32[si], psum[bu][:, :], ALU.mult,
                         reads=[b_w32[si], b_ps[bu]], writes=[b_r1[cc][t]])
                w_release(pw)
            for n in range(8):
                pw, sw = w_get()
                for t in range(NT):
                    bank = next_ps()
                    for kc in range(11):
                        mm(psum[bank][:, :], wsl[:, sw, kc * 128:(kc + 1) * 128], r1[:, kc, tsl(t)],
                           kc == 0, kc == 10, reads=[b_ws[sw], b_r1[kc][t]], writes=[b_ps[bank]], inc=(kc == 10))
                    v_tt(xT[:, n, tsl(t)], psum[bank][:, :], xT[:, n, tsl(t)], ALU.add,
                         reads=[b_ps[bank], b_xT[n][t]], writes=[b_xT[n][t]])
                w_release(pw)

      except _Stop:
        break

    for c in range(8):
        SP.dma(lambda e, c=c: e.dma_start(out=d_out[:, c, :], in_=xT[:, c, :]), ch_o,
               reads=[b_xT[c][t] for t in range(NT)])
    SP.wait_chan(ch_o, ch_o.count)
    assert stop_after is not None or wstate["got"] == len(pieces)

    with nc.Block() as block:
        @block.tensor
        def _(e):
            for f in PE.ops:
                f(e)

        @block.scalar
        def _(e):
            for f in ACT.ops:
                f(e)

        @block.vector
        def _(e):
            for f in DVE.ops:
                f(e)

        @block.gpsimd
        def _(e):
            for f in POOL.ops:
                f(e)

        @block.sync
        def _(e):
            for f in SP.ops:
                f(e)
    return nc


def _kc_layout(w):
    K, N = w.shape
    return np.ascontiguousarray(w.reshape(K // 128, 128, N).transpose(1, 0, 2)).reshape(128, (K // 128) * N)


def _prep_weights(w_in, w_out, w_gate_up, w_down, NL):
    out = np.empty((NL, 128, WTOT), np.float32)
    for l in range(NL):
        parts = []
        wi = w_in[l]
        cols = [wi[:, 0:256], wi[:, 256:512], wi[:, 768:1024], wi[:, 512:768]]
        qk = lambda hp: np.concatenate([wi[:, 1024 + hp * 128:1024 + (hp + 1) * 128],
                                        wi[:, 1536 + hp * 128:1536 + (hp + 1) * 128]], axis=1)
        cols += [qk(0), wi[:, 2048:2304], qk(1), qk(2), wi[:, 2304:2560], qk(3)]
        for c in cols:
            parts.append(_kc_layout(c))
        for j in range(4):
            parts.append(_kc_layout(w_out[l][:, j * 256:(j + 1) * 256]))
        wgu = w_gate_up[l]
        wd = w_down[l]
        for g in range(2):
            for cc in range(11):
                c = g * 11 + cc
                parts.append(_kc_layout(np.concatenate([wgu[:, c * 128:(c + 1) * 128],
                                                        wgu[:, HID + c * 128:HID + (c + 1) * 128]], axis=1)))
            for n in range(8):
                parts.append(_kc_layout(wd[g * 1408:(g + 1) * 1408, n * 128:(n + 1) * 128]))
        out[l] = np.concatenate(parts, axis=1)
    return out


def _prep_params(inp, NL):
    p = np.arange(128)
    pcol = np.empty((128, NL * NPCOL), np.float32)
    pmisc = np.empty((NL, 128, PMISC), np.float32)
    for l in range(NL):
        o = l * NPCOL
        fm = lambda v: v.reshape(-1, 128).T
        pcol[:, o + 0:o + 8] = fm(inp["mix_norm_g"][l])
        pcol[:, o + 8:o + 16] = fm(inp["ffn_norm_g"][l])
        pcol[:, o + 16:o + 24] = fm(inp["out_norm_g"][l])
        cw = inp["conv_w"][l]
        for fc in range(2):
            pcol[:, o + 24 + fc * 31:o + 24 + (fc + 1) * 31] = cw[:, fc * 128:(fc + 1) * 128].T
        pcol[:, o + 86:o + 88] = fm(inp["conv_b"][l])
        pcol[:, o + 88:o + 90] = fm(inp["conv_ln_g"][l])
        pcol[:, o + 90:o + 92] = fm(inp["conv_ln_b"][l])
        pcol[:, o + 92] = inp["q_norm_g"][l][p % 64]
        pcol[:, o + 93] = inp["k_norm_g"][l][p % 64]
        pmisc[l, :, 0:256] = inp["sg_ln_g"][l][None, :]
        pmisc[l, :, 256:512] = inp["sg_ln_b"][l][None, :]
        sgb = inp["sg_b"][l]
        for fc in range(2):
            for rep in range(4):
                base = 512 + fc * 512 + rep * 128
                pmisc[l, 0:64, base:base + 128] = sgb[2 * fc][None, :]
                pmisc[l, 64:128, base:base + 128] = sgb[2 * fc + 1][None, :]
        sgw = inp["sg_w"][l]
        pmisc[l, :, 1536:2048] = sgw.transpose(2, 0, 1).reshape(128, 512)
    return pcol, pmisc


def _consts():
    j = np.arange(128)[:, None]
    s = np.arange(128)[None, :]
    c = np.zeros((128, 8, 128), np.float32)
    c[:, 0, :] = 1.0
    c[:, 1, :] = ((j // 64) == (s // 64)).astype(np.float32)
    c[:, 2, :] = -(j >= s).astype(np.float32)
    c[:, 3, :] = -1.0
    c[:, 4, :] = (j < s).astype(np.float32)
    c[:, 5, :] = (j <= s).astype(np.float32)
    c[:, 6, :] = (j == s).astype(np.float32)
    c[:, 7, :] = -30000.0 * (j >= s).astype(np.float32)
    return c.astype(ml_dtypes.bfloat16)


_PROGRAM_CACHE = {}


def _run(inp, NL):
    x = np.asarray(inp["x"], np.float32)
    B = x.shape[0]
    f = {k: np.asarray(v, np.float32) for k, v in inp.items()}
    wflat = _prep_weights(f["w_in"], f["w_out"], f["w_gate_up"], f["w_down"], NL)
    pcol, pmisc = _prep_params(f, NL)
    cst = _consts()
    if NL not in _PROGRAM_CACHE:
        _PROGRAM_CACHE[NL] = build_program(NL)
    nc = _PROGRAM_CACHE[NL]
    in_maps = []
    for b in range(B):
        xT = np.ascontiguousarray(x[b].T.reshape(8, 128, S).transpose(1, 0, 2))
        m = {"xT": xT, "pcol": pcol, "pmisc": pmisc, "cst": cst}
        for l in range(NL):
            m[f"wflat{l}"] = wflat[l]
        in_maps.append(m)
    res = run_bass_kernel_spmd(nc, in_maps, core_ids=list(range(B)))
    out = np.empty((B, S, D), np.float32)
    for b in range(B):
        oT = np.asarray(res.results[b]["outT"], np.float32)
        out[b] = oT.transpose(1, 0, 2).reshape(D, S).T
    return out


def kernel(**inputs):
    return _run(inputs, L_FULL)
```

```python
import numpy as np
import ml_dtypes
import concourse.bass as bass
import concourse.mybir as mybir
from concourse.bass_utils import run_bass_kernel_spmd

F32 = mybir.dt.float32
BF16 = mybir.dt.bfloat16
AF = mybir.ActivationFunctionType
ALU = mybir.AluOpType

S = 2048
D = 1024
L_FULL = 4
NT = 4
TT = 512
HID = 2816
NSLOT = 5
MAX_INFLIGHT = 1
WSZ = 2048
RMS_EPS = 1e-6
LN_EPS = 1e-5
WTOT = 36 * 2048 + 16 * 1408
NPCOL = 94
PMISC = 2048

SAME_ENGINE_SYNC = True
ATTACH_WAITS = True


class Chan:
    def __init__(self, sem, name):
        self.sem = sem
        self.count = 0
        self.name = name


class Buf:
    __slots__ = ("name", "w", "readers", "excl")

    def __init__(self, name, excl=False):
        self.name = name
        self.w = None
        self.readers = {}
        self.excl = excl


class Eng:
    def __init__(self, name, chan):
        self.name = name
        self.chan = chan
        self.ops = []
        self.waited = {}
        self.pend_r = []
        self.pend_w = []

    def _waits(self, reads, writes):
        need = {}

        def req(tag):
            ch, val = tag
            if ch is self.chan and (self.name == "pe" or not SAME_ENGINE_SYNC):
                return
            if need.get(ch, 0) < val:
                need[ch] = val

        for b in reads:
            if b.w is not None:
                req(b.w)
        for b in writes:
            if b.w is not None:
                req(b.w)
            for ch, val in b.readers.items():
                req((ch, val))
        todo = []
        for ch, val in need.items():
            if self.waited.get(ch, 0) < val:
                self.waited[ch] = val
                todo.append((ch.sem, val))
        self.attach = None
        if todo and self.name in ("act", "dve") and ATTACH_WAITS:
            self.attach = todo.pop()
        for sem, val in todo:
            self.ops.append(lambda e, s=sem, v=val: e.wait_ge(s, v))

    def op(self, fn, reads=(), writes=(), inc=True):
        if any(b.excl for b in reads):
            writes = list(writes) + [b for b in reads if b.excl]
            reads = [b for b in reads if not b.excl]
        self._waits(reads, writes)
        if not inc:
            if getattr(self, "attach", None) is not None:
                a = self.attach
                self.attach = None
                self.ops.append(lambda e, a=a: e.wait_ge(a[0], a[1]))
            self.pend_r.extend(reads)
            self.pend_w.extend(writes)
            self.ops.append(lambda e: fn(e))
            return
        self.chan.count += 1
        tag = (self.chan, self.chan.count)
        att = getattr(self, "attach", None)
        self.attach = None
        if att is not None:
            self.ops.append(lambda e, s=self.chan.sem, a=att: fn(e)._wait_ge(a[0], a[1]).then_inc(s, 1))
        else:
            self.ops.append(lambda e, s=self.chan.sem: fn(e).then_inc(s, 1))
        for b in list(reads) + self.pend_r:
            if b.readers.get(tag[0], 0) < tag[1]:
                b.readers[tag[0]] = tag[1]
        for b in list(writes) + self.pend_w:
            b.w = tag
            b.readers = {}
        self.pend_r = []
        self.pend_w = []

    def dma(self, fn, chan, reads=(), writes=()):
        self._waits(reads, writes)
        chan.count += 16
        tag = (chan, chan.count)
        self.ops.append(lambda e, s=chan.sem: fn(e).then_inc(s, 16))
        for b in reads:
            if b.readers.get(chan, 0) < tag[1]:
                b.readers[chan] = tag[1]
        for b in writes:
            b.w = tag
            b.readers = {}

    def wait_chan(self, chan, val):
        if self.waited.get(chan, 0) < val:
            self.waited[chan] = val
            self.ops.append(lambda e, s=chan.sem, v=val: e.wait_ge(s, v))


class _Stop(Exception):
    pass


def build_program(NL, stop_after=None):
    nc = bass.Bass("TRN2", target_bir_lowering=False)

    d_xT = nc.dram_tensor("xT", [128, 8, S], F32, kind="ExternalInput").ap()
    d_wl = [nc.dram_tensor(f"wflat{l}", [128, WTOT], F32, kind="ExternalInput").ap() for l in range(NL)]
    d_pcol = nc.dram_tensor("pcol", [128, NL * NPCOL], F32, kind="ExternalInput").ap()
    d_pmisc = nc.dram_tensor("pmisc", [NL, 128, PMISC], F32, kind="ExternalInput").ap()
    d_cst = nc.dram_tensor("cst", [128, 8, 128], BF16, kind="ExternalInput").ap()
    d_out = nc.dram_tensor("outT", [128, 8, S], F32, kind="ExternalOutput").ap()

    def mkchan(name):
        return Chan(nc.alloc_semaphore(name), name)

    PE = Eng("pe", mkchan("c_pe"))
    ACT = Eng("act", mkchan("c_act"))
    DVE = Eng("dve", mkchan("c_dve"))
    POOL = Eng("pool", mkchan("c_pool"))
    SP = Eng("sp", mkchan("c_sp"))
    engines = [PE, ACT, DVE, POOL, SP]

    def barrier():
        for e in (PE, ACT, DVE):
            for f in (PE, ACT, DVE):
                if f is not e and f.chan.count > 0:
                    e.wait_chan(f.chan, f.chan.count)

    xT = nc.alloc_sbuf_tensor("xT_sb", [128, 8, S], F32)
    hT = nc.alloc_sbuf_tensor("hT_sb", [128, 8, S], BF16)
    r1 = nc.alloc_sbuf_tensor("r1_sb", [128, 11, S], BF16)
    wsl = nc.alloc_sbuf_tensor("wslots", [128, NSLOT, WSZ], BF16)
    wk32 = nc.alloc_sbuf_tensor("wk32", [128, 4, TT], F32)
    wk16 = nc.alloc_sbuf_tensor("wk16", [128, 8, TT], BF16)
    PHW = 2 * (30 + S)
    ph = nc.alloc_sbuf_tensor("phase", [128, PHW], F32)
    cst = nc.alloc_sbuf_tensor("cst_sb", [128, 8, 128], BF16)
    pcol = nc.alloc_sbuf_tensor("pcol_sb", [128, NL * NPCOL], F32)
    pmisc = nc.alloc_sbuf_tensor("pmisc_sb", [128, PMISC], F32)
    sgw16 = nc.alloc_sbuf_tensor("sgw16", [128, 4, 128], BF16)
    small = nc.alloc_sbuf_tensor("small", [128, 64], F32)
    qgs = nc.alloc_sbuf_tensor("qgs", [128, NL], F32)
    cacc = nc.alloc_sbuf_tensor("cacc", [128, TT], F32)

    hc = ph[:, :].rearrange("p (c t) -> p c t", c=2)
    SS = pmisc[:, 0:S]

    psall = nc.alloc_psum_tensor("psall", [128, 8, TT], F32)
    psum = [psall[:, i, :] for i in range(8)]

    b_xT = [[Buf(f"xT{c}_{t}") for t in range(NT)] for c in range(8)]
    b_hT = [[Buf(f"hT{c}_{t}") for t in range(NT)] for c in range(8)]
    b_r1 = [[Buf(f"r1{c}_{t}") for t in range(NT)] for c in range(11)]
    b_ws = [Buf(f"ws{i}") for i in range(NSLOT)]
    b_w32 = [Buf(f"wk32_{i}") for i in range(4)]
    b_w16 = [Buf(f"wk16_{i}") for i in range(8)]
    b_ps = [Buf(f"ps{i}", excl=True) for i in range(8)]
    b_hc = [[Buf(f"hc{c}_{t}") for t in range(NT)] for c in range(2)]
    b_hcpad = Buf("hcpad")
    b_SS = [Buf(f"SS{t}") for t in range(NT)]
    b_cst = Buf("cst")
    b_pcol = Buf("pcol")
    b_pmisc = Buf("pmisc")
    b_sgw16 = Buf("sgw16")
    b_small = Buf("small")
    b_qgs = Buf("qgs")
    b_cacc = Buf("cacc")

    ch_ws = [mkchan(f"c_ws{i}") for i in range(NSLOT)]
    ch_x = mkchan("c_x")
    ch_c = mkchan("c_cst")
    ch_m = mkchan("c_misc")
    ch_o = mkchan("c_out")

    def tsl(t):
        return slice(t * TT, (t + 1) * TT)

    W32 = [wk32[:, i, :] for i in range(4)]
    W16 = [wk16[:, i, :] for i in range(8)]
    ONES = cst[:, 0, :]
    BD64 = cst[:, 1, :]
    NEGU = cst[:, 2, :]
    NEGONES = cst[:, 3, :]
    MASK_LT = cst[:, 4, :]
    MASK_LE = cst[:, 5, :]
    IDENT = cst[:, 6, :]
    NEGMASK = cst[:, 7, :]

    ps_rr = [0]

    def next_ps():
        i = ps_rr[0]
        ps_rr[0] = (i + 1) % 6
        return i

    def next_ps_pair():
        i = ps_rr[0]
        if i % 2:
            i = (i + 1) % 6
        ps_rr[0] = (i + 2) % 6
        return i

    def mm(out, lhsT, rhs, start, stop, reads, writes, inc, skip=False):
        PE.op(lambda e: e.matmul(out, lhsT=lhsT, rhs=rhs, start=start, stop=stop, skip_group_check=skip),
              reads=reads, writes=writes, inc=inc)

    def act(out, in_, func, reads, writes, bias=None, scale=None):
        kw = {}
        if bias is not None:
            kw["bias"] = bias
        if scale is not None:
            kw["scale"] = scale
        ACT.op(lambda e: e.activation(out=out, in_=in_, func=func, **kw), reads=reads, writes=writes)

    def v_tt(out, in0, in1, op, reads, writes):
        DVE.op(lambda e: e.tensor_tensor(out=out, in0=in0, in1=in1, op=op), reads=reads, writes=writes)

    def v_ts(out, in0, s1, s2, op0, op1, reads, writes):
        if s2 is None:
            DVE.op(lambda e: e.tensor_scalar(out=out, in0=in0, scalar1=s1, scalar2=None, op0=op0),
                   reads=reads, writes=writes)
        else:
            DVE.op(lambda e: e.tensor_scalar(out=out, in0=in0, scalar1=s1, scalar2=s2, op0=op0, op1=op1),
                   reads=reads, writes=writes)

    def v_stt(out, in0, scalar, in1, op0, op1, reads, writes):
        DVE.op(lambda e: e.scalar_tensor_tensor(out=out, in0=in0, scalar=scalar, in1=in1, op0=op0, op1=op1),
               reads=reads, writes=writes)

    def v_copy(out, in_, reads, writes):
        DVE.op(lambda e: e.tensor_copy(out=out, in_=in_), reads=reads, writes=writes)

    def rstd_from(ps_ap, scale, eps, tmp_i, out_i, rd_bufs):
        act(W32[tmp_i], ps_ap, AF.Ln, reads=rd_bufs, writes=[b_w32[tmp_i]], bias=eps, scale=scale)
        act(W32[out_i], W32[tmp_i], AF.Exp, reads=[b_w32[tmp_i]], writes=[b_w32[out_i]], scale=-0.5)

    pieces = []
    for l in range(NL):
        off = 0
        for _ in range(14):
            pieces.append((l, off, 2048)); off += 2048
        for g in range(2):
            for _ in range(11):
                pieces.append((l, off, 2048)); off += 2048
            for _ in range(8):
                pieces.append((l, off, 1408)); off += 1408
        assert off == WTOT
    wstate = {"issued": 0, "got": 0, "free": list(range(NSLOT)), "slot_of": {}}

    def w_issue():
        while wstate["free"] and wstate["issued"] < len(pieces):
            i = wstate["issued"]
            sl = wstate["free"].pop(0)
            l, off, sz = pieces[i]
            hist = wstate.setdefault("hist", [])
            if len(hist) >= MAX_INFLIGHT:
                pch, pval = hist[-MAX_INFLIGHT]
                POOL.wait_chan(pch, pval)
            hist.append((ch_ws[sl], ch_ws[sl].count + 16))
            POOL.dma(lambda e, sl=sl, l=l, off=off, sz=sz: e.dma_start(out=wsl[:, sl, 0:sz], in_=d_wl[l][:, off:off + sz]),
                     ch_ws[sl], reads=(), writes=[b_ws[sl]])
            wstate["slot_of"][i] = sl
            wstate["issued"] += 1

    def w_get():
        i = wstate["got"]
        assert i in wstate["slot_of"], "weight piece not issued (too many live pieces)"
        wstate["got"] += 1
        return i, wstate["slot_of"][i]

    def w_release(i):
        sl = wstate["slot_of"].pop(i)
        wstate["free"].append(sl)
        w_issue()

    SP.dma(lambda e: e.dma_start(out=cst[:], in_=d_cst), ch_c, writes=[b_cst])
    SP.dma(lambda e: e.dma_start(out=pcol[:], in_=d_pcol), ch_c, writes=[b_pcol])
    for c in range(8):
        SP.dma(lambda e, c=c: e.dma_start(out=xT[:, c, :], in_=d_xT[:, c, :]), ch_x,
               writes=[b_xT[c][t] for t in range(NT)])
    for c in range(8):
        for t in range(NT):
            b_xT[c][t].w = (ch_x, ch_x.count)
    b_cst.w = (ch_c, ch_c.count)
    b_pcol.w = (ch_c, ch_c.count)
    w_issue()

    for l in range(NL):
        v_ts(qgs[:, l:l + 1], pcol[:, l * NPCOL + 92:l * NPCOL + 93], 0.125, None, ALU.mult, None,
             reads=[b_pcol], writes=[b_qgs])

    def pc(l, j):
        return pcol[:, l * NPCOL + j:l * NPCOL + j + 1]

    def rmsnorm_to_hT(l, goff):
        for t in range(NT):
            bank = next_ps()
            for c in range(8):
                si = c % 4
                act(W16[si], xT[:, c, tsl(t)], AF.Square, reads=[b_xT[c][t]], writes=[b_w16[si]])
                mm(psum[bank][:, :], ONES, W16[si], c == 0, c == 7,
                   reads=[b_w16[si], b_cst], writes=[b_ps[bank]], inc=True)
            rstd_from(psum[bank][:, :], 1.0 / D, RMS_EPS, 0, 1, [b_ps[bank]])
            for c in range(8):
                v_stt(hT[:, c, tsl(t)], xT[:, c, tsl(t)], pc(l, goff + c), W32[1], ALU.mult, ALU.mult,
                      reads=[b_xT[c][t], b_w32[1], b_pcol], writes=[b_hT[c][t]])

    def proj_fm(bank, slot, col0, t, last_inc=True):
        for kc in range(8):
            mm(psum[bank][:, :], wsl[:, slot, kc * 256 + col0: kc * 256 + col0 + 128], hT[:, kc, tsl(t)],
               kc == 0, kc == 7, reads=[b_ws[slot], b_hT[kc][t]], writes=[b_ps[bank]],
               inc=(kc == 7))

    def phase(k):
        if stop_after is not None and k >= stop_after:
            raise _Stop()

    for l in range(NL):
      try:
        SP.dma(lambda e, l=l: e.dma_start(out=pmisc[:], in_=d_pmisc[l]), ch_m, writes=[b_pmisc] + b_SS)
        for h in range(4):
            v_tt(sgw16[:, h, :], pmisc[:, 1536 + h * 128:1536 + (h + 1) * 128], MASK_LE, ALU.mult,
                 reads=[b_pmisc, b_cst], writes=[b_sgw16])
        DVE.op(lambda e: e.memset(hc[:, :, 0:30], 0.0), writes=[b_hcpad])

        phase(0)
        rmsnorm_to_hT(l, 0)
        phase(1)

        pv, sv = w_get()
        pg, sg_ = w_get()
        for fc in range(2):
            for t in range(NT):
                bv = next_ps()
                proj_fm(bv, sv, fc * 128, t)
                bg = next_ps()
                proj_fm(bg, sg_, fc * 128, t)
                act(W32[0], psum[bg][:, :], AF.Sigmoid, reads=[b_ps[bg]], writes=[b_w32[0]])
                v_tt(hc[:, fc, 30 + t * TT:30 + (t + 1) * TT], psum[bv][:, :], W32[0], ALU.mult,
                     reads=[b_ps[bv], b_w32[0]], writes=[b_hc[fc][t]])
        w_release(pv)
        w_release(pg)
        phase(2)
        fillers = []
        for fc in range(2):
            for t in range(NT - 1, -1, -1):
                rd = [b_hc[fc][t], b_hcpad, b_pcol] + ([b_hc[fc][t - 1]] if t > 0 else [])
                cw0 = 24 + fc * 31
                fillers.append(lambda fc=fc, t=t, rd=rd, cw0=cw0: v_ts(
                    cacc[:, :], hc[:, fc, t * TT:t * TT + TT], pc(l, cw0), pc(l, 86 + fc), ALU.mult, ALU.add,
                    reads=rd, writes=[b_cacc]))
                for k in range(1, 30):
                    fillers.append(lambda fc=fc, t=t, rd=rd, cw0=cw0, k=k: v_stt(
                        cacc[:, :], hc[:, fc, t * TT + k:t * TT + k + TT], pc(l, cw0 + k), cacc[:, :], ALU.mult, ALU.add,
                        reads=rd + [b_cacc], writes=[b_cacc]))
                fillers.append(lambda fc=fc, t=t, rd=rd, cw0=cw0: v_stt(
                    hc[:, fc, 30 + t * TT:30 + t * TT + TT], hc[:, fc, 30 + t * TT:30 + t * TT + TT],
                    pc(l, cw0 + 30), cacc[:, :], ALU.mult, ALU.add,
                    reads=rd + [b_cacc], writes=[b_hc[fc][t]]))

        def run_fillers(n):
            for _ in range(n):
                if fillers:
                    fillers.pop(0)()

        phase(4)
        pvv, svv = w_get()
        pu, su = w_get()
        for t in range(NT):
            for sub in range(4):
                ck = 4 * t + sub
                bank = next_ps()
                for kc in range(8):
                    mm(psum[bank][:, 0:256], hT[:, kc, ck * 128:(ck + 1) * 128], wsl[:, svv, kc * 256:(kc + 1) * 256],
                       kc == 0, kc == 7, reads=[b_ws[svv], b_hT[kc][t]], writes=[b_ps[bank]], inc=(kc == 7))
                wi = sub // 2
                vg = wk32[:, wi, (sub % 2) * 256:(sub % 2) * 256 + 256]
                act(vg, psum[bank][:, 0:256], AF.Gelu, reads=[b_ps[bank]], writes=[b_w32[wi]])
                DVE.op(lambda e, vg=vg, sub=sub: e.bn_stats(out=small[:, sub * 6:sub * 6 + 6], in_=vg),
                       reads=[b_w32[wi]], writes=[b_small])
                DVE.op(lambda e, sub=sub: e.bn_aggr(out=small[:, 24 + sub * 2:24 + sub * 2 + 2],
                                                    in_=small[:, sub * 6:sub * 6 + 6]),
                       reads=[b_small], writes=[b_small])
            var4 = small[:, 24:32].rearrange("p (a b) -> p a b", b=2)[:, :, 1]
            act(small[:, 32:36], var4, AF.Ln, reads=[b_small], writes=[b_small], bias=LN_EPS)
            act(small[:, 36:40], small[:, 32:36], AF.Exp, reads=[b_small], writes=[b_small], scale=-0.5)
            for sub in range(4):
                wi = sub // 2
                vg = wk32[:, wi, (sub % 2) * 256:(sub % 2) * 256 + 256]
                v_ts(vg, vg, small[:, 24 + sub * 2:25 + sub * 2], small[:, 36 + sub:37 + sub], ALU.subtract, ALU.mult,
                     reads=[b_w32[wi], b_small], writes=[b_w32[wi]])
                v_tt(vg, vg, pmisc[:, 0:256], ALU.mult, reads=[b_w32[wi], b_pmisc], writes=[b_w32[wi]])
                vh = wk16[:, wi, (sub % 2) * 256:(sub % 2) * 256 + 256]
                v_tt(vh, vg, pmisc[:, 256:512], ALU.add, reads=[b_w32[wi], b_pmisc], writes=[b_w16[wi]])
            for fc in range(2):
                bank = next_ps()
                proj_fm(bank, su, fc * 128, t)
                act(W32[fc], psum[bank][:, :], AF.Gelu, reads=[b_ps[bank]], writes=[b_w32[fc]])
            for fc in range(2):
                bank = next_ps()
                for sub in range(4):
                    for hh in range(2):
                        h = 2 * fc + hh
                        wi = sub // 2
                        vh = wk16[:, wi, (sub % 2) * 256 + h * 64:(sub % 2) * 256 + h * 64 + 64]
                        mm(psum[bank][hh * 64:(hh + 1) * 64, sub * 128:(sub + 1) * 128], vh, sgw16[:, h, :],
                           True, True, reads=[b_w16[wi], b_sgw16], writes=[b_ps[bank]],
                           inc=(sub == 3 and hh == 1))
                v_tt(W32[2 + fc], psum[bank][:, :], pmisc[:, 512 + fc * 512:512 + (fc + 1) * 512], ALU.add,
                     reads=[b_ps[bank], b_pmisc], writes=[b_w32[2 + fc]])
                v_tt(W32[2 + fc], W32[2 + fc], W32[fc], ALU.mult, reads=[b_w32[2 + fc], b_w32[fc]],
                     writes=[b_w32[2 + fc]])
                act(W16[2 + fc], W32[2 + fc], AF.Square, reads=[b_w32[2 + fc]], writes=[b_w16[2 + fc]])
            b3 = next_ps()
            for fc in range(2):
                mm(psum[b3][:, :], ONES, W16[2 + fc], fc == 0, fc == 1, reads=[b_w16[2 + fc], b_cst],
                   writes=[b_ps[b3]], inc=True)
            rstd_from(psum[b3][:, :], 1.0 / 256, RMS_EPS, 0, 1, [b_ps[b3]])
            for fc in range(2):
                v_stt(r1[:, 2 + fc, tsl(t)], W32[2 + fc], pc(l, 18 + fc), W32[1], ALU.mult, ALU.mult,
                      reads=[b_w32[2 + fc], b_w32[1], b_pcol], writes=[b_r1[2 + fc][t]])
        w_release(pvv)
        w_release(pu)

        phase(5)
        qT = r1[:, 8, :]
        kT = r1[:, 9, :]
        v16 = r1[:, 10, :].rearrange("p (c d) -> p c d", d=128)
        pieces_v = {}
        for hp in range(4):
            pqk, sqk = w_get()
            if hp % 2 == 0:
                pieces_v["cur"] = w_get()
            pvp, svp = pieces_v["cur"]
            for t in range(NT):
                for which in range(2):
                    bank = next_ps()
                    proj_fm(bank, sqk, which * 128, t)
                    v_copy(W32[0], psum[bank][:, :], reads=[b_ps[bank]], writes=[b_w32[0]])
                    act(W16[0], psum[bank][:, :], AF.Square, reads=[b_ps[bank]], writes=[b_w16[0]])
                    b2 = next_ps()
                    mm(psum[b2][:, :], BD64, W16[0], True, True, reads=[b_w16[0], b_cst], writes=[b_ps[b2]], inc=True)
                    rstd_from(psum[b2][:, :], 1.0 / 64, RMS_EPS, 1, 2, [b_ps[b2]])
                    gcol = qgs[:, l:l + 1] if which == 0 else pc(l, 93)
                    dst = (qT if which == 0 else kT)[:, tsl(t)]
                    v_stt(dst, W32[0], gcol, W32[2], ALU.mult, ALU.mult,
                          reads=[b_w32[0], b_w32[2], b_qgs, b_pcol], writes=[b_r1[8 + which][t]])
            phase(5.2)
            for t in range(NT):
                bank = next_ps()
                for sub in range(4):
                    ck = 4 * t + sub
                    for kc in range(8):
                        c0 = kc * 256 + (hp % 2) * 128
                        mm(psum[bank][:, sub * 128:(sub + 1) * 128], hT[:, kc, ck * 128:(ck + 1) * 128],
                           wsl[:, svp, c0:c0 + 128], kc == 0, kc == 7,
                           reads=[b_ws[svp], b_hT[kc][t]], writes=[b_ps[bank]], inc=(kc == 7 and sub == 3))
                v_copy(r1[:, 10, tsl(t)], psum[bank][:, :], reads=[b_ps[bank]], writes=[b_r1[10][t]])
            w_release(pqk)
            if hp % 2 == 1:
                w_release(pvp)
            phase(5.4)
            iters = [(qt, kb) for qt in range(NT) for kb in range(4 * qt + 3, -1, -1)]
            ctxs = {}

            def setup(i):
                qt, kb = iters[i]
                ob = 6 + (qt % 2)
                diag = kb >= 4 * qt
                c0 = (kb - 4 * qt) * 128 if diag else 0
                kt = kb // 4
                cs = slice(c0, TT)
                first = (kb == 4 * qt + 3)
                cc0 = c0 + 128 if diag else 0
                has_carry = (not first) and cc0 < TT
                par = kb % 2
                zb = next_ps_pair()
                ctxs[i] = (qt, kb, ob, diag, c0, kt, cs, first, cc0, has_carry, par, zb)

            def stage_a1(i):
                qt, kb, ob, diag, c0, kt, cs, first, cc0, has_carry, par, zb = ctxs[i]
                bz = [b_ps[zb], b_ps[zb + 1]]
                bE = [b_w32[0], b_w32[1]]
                for hh in range(2):
                    hs = slice(hh * 64, hh * 64 + 64)
                    mm(psall[:, zb + hh, cs], kT[hs, kb * 128:(kb + 1) * 128], qT[hs, qt * TT + c0:(qt + 1) * TT],
                       True, True, reads=[b_r1[9][kt], b_r1[8][qt]], writes=[b_ps[zb + hh]], inc=not diag)
                    if diag:
                        mm(psall[:, zb + hh, c0:c0 + 128], IDENT, NEGMASK, False, True,
                           reads=[b_cst], writes=[b_ps[zb + hh]], inc=True, skip=True)
                act(wk32[:, 0:2, cs], psall[:, zb:zb + 2, cs], AF.Exp, reads=bz, writes=bE)

            def stage_a2(i):
                qt, kb, ob, diag, c0, kt, cs, first, cc0, has_carry, par, zb = ctxs[i]
                L0 = 2 * par
                bE = [b_w32[0], b_w32[1]]
                bL = [b_w16[L0], b_w16[L0 + 1]]
                act(wk16[:, L0:L0 + 2, cs], wk32[:, 0:2, cs], AF.Ln, reads=bE, writes=bL, bias=1.0)

            def stage_b1(i):
                qt, kb, ob, diag, c0, kt, cs, first, cc0, has_carry, par, zb = ctxs[i]
                L0 = 2 * par
                bz = [b_ps[zb], b_ps[zb + 1]]
                bL = [b_w16[L0], b_w16[L0 + 1]]
                bA = [b_w16[4], b_w16[5]]
                bR = [b_w32[2], b_w32[3]]
                bR16 = [b_w16[6], b_w16[7]]
                for hh in range(2):
                    mm(psall[:, zb + hh, cs], NEGU, wk16[:, L0 + hh, cs], False, not has_carry,
                       reads=[b_w16[L0 + hh], b_cst], writes=[b_ps[zb + hh]], inc=not has_carry, skip=True)
                    if has_carry:
                        mm(psall[:, zb + hh, cc0:TT], NEGONES, wk16[:, 6 + hh, cc0:TT], False, True,
                           reads=[b_w16[6 + hh], b_cst], writes=[b_ps[zb + hh]], inc=True, skip=True)
                act(wk16[:, 4:6, cs], psall[:, zb:zb + 2, cs], AF.Exp, reads=bz, writes=bA)

            def stage_b2(i):
                qt, kb, ob, diag, c0, kt, cs, first, cc0, has_carry, par, zb = ctxs[i]
                L0 = 2 * par
                bL = [b_w16[L0], b_w16[L0 + 1]]
                bR = [b_w32[2], b_w32[3]]
                bR16 = [b_w16[6], b_w16[7]]
                for hh in range(2):
                    hs = slice(hh * 64, hh * 64 + 64)
                    mm(psum[ob][hs, cs], v16[:, kb, hs], wk16[:, 4 + hh, cs], first, kb == 0,
                       reads=[b_r1[10][kt], b_w16[4 + hh]], writes=[b_ps[ob]], inc=True, skip=True)
                if kb > 0:
                    if diag:
                        v_copy(wk32[:, 2:4, c0:c0 + 128], wk16[:, L0:L0 + 2, c0:c0 + 128], reads=bL, writes=bR)
                        if c0 + 128 < TT:
                            v_tt(wk32[:, 2:4, c0 + 128:TT], wk32[:, 2:4, c0 + 128:TT], wk16[:, L0:L0 + 2, c0 + 128:TT],
                                 ALU.add, reads=bR + bL, writes=bR)
                    else:
                        v_tt(wk32[:, 2:4, :], wk32[:, 2:4, :], wk16[:, L0:L0 + 2, :], ALU.add,
                             reads=bR + bL, writes=bR)
                    v_copy(wk16[:, 6:8, cs], wk32[:, 2:4, cs], reads=bR, writes=bR16)
                if kb == 0:
                    phase(5.5)
                    v_copy(r1[:, 4 + hp, tsl(qt)], psum[ob][:, :], reads=[b_ps[ob]], writes=[b_r1[4 + hp][qt]])
                    act(W16[0], psum[ob][:, :], AF.Square, reads=[b_ps[ob]], writes=[b_w16[0]])
                    b2 = next_ps()
                    mm(psum[b2][:, :], ONES, W16[0], True, True, reads=[b_w16[0], b_cst], writes=[b_ps[b2]], inc=True)
                    if hp == 0:
                        v_copy(SS[:, tsl(qt)], psum[b2][:, :], reads=[b_ps[b2]], writes=[b_SS[qt], b_pmisc])
                    else:
                        v_tt(SS[:, tsl(qt)], SS[:, tsl(qt)], psum[b2][:, :], ALU.add,
                             reads=[b_ps[b2], b_SS[qt]], writes=[b_SS[qt], b_pmisc])

            setup(0)
            stage_a1(0)
            stage_a2(0)
            for i in range(len(iters)):
                nxt = i + 1 < len(iters)
                if nxt:
                    setup(i + 1)
                    stage_a1(i + 1)
                stage_b1(i)
                if nxt:
                    stage_a2(i + 1)
                stage_b2(i)
                run_fillers(2)
        run_fillers(len(fillers))
        phase(3)
        for t in range(NT):
            b1 = next_ps()
            b2 = next_ps()
            for fc in range(2):
                cv = hc[:, fc, 30 + t * TT:30 + (t + 1) * TT]
                act(W16[fc], cv, AF.Copy, reads=[b_hc[fc][t]], writes=[b_w16[fc]])
                act(W16[2 + fc], cv, AF.Square, reads=[b_hc[fc][t]], writes=[b_w16[2 + fc]])
            for fc in range(2):
                mm(psum[b1][:, :], ONES, W16[fc], fc == 0, fc == 1, reads=[b_w16[fc], b_cst],
                   writes=[b_ps[b1]], inc=True)
            for fc in range(2):
                mm(psum[b2][:, :], ONES, W16[2 + fc], fc == 0, fc == 1, reads=[b_w16[2 + fc], b_cst],
                   writes=[b_ps[b2]], inc=True)
            v_ts(W32[0], psum[b1][:, :], 1.0 / 256, None, ALU.mult, None, reads=[b_ps[b1]], writes=[b_w32[0]])
            v_tt(W32[1], W32[0], W32[0], ALU.mult, reads=[b_w32[0]], writes=[b_w32[1]])
            v_stt(W32[1], psum[b2][:, :], 1.0 / 256, W32[1], ALU.mult, ALU.subtract,
                  reads=[b_ps[b2], b_w32[1]], writes=[b_w32[1]])
            rstd_from(W32[1], 1.0, LN_EPS, 2, 3, [b_w32[1]])
            for fc in range(2):
                cv = hc[:, fc, 30 + t * TT:30 + (t + 1) * TT]
                v_tt(cv, cv, W32[0], ALU.subtract, reads=[b_hc[fc][t], b_w32[0]], writes=[b_hc[fc][t]])
                v_tt(cv, cv, W32[3], ALU.mult, reads=[b_hc[fc][t], b_w32[3]], writes=[b_hc[fc][t]])
                act(cv, cv, AF.Silu, reads=[b_hc[fc][t], b_pcol], writes=[b_hc[fc][t]],
                    bias=pc(l, 90 + fc), scale=pc(l, 88 + fc))
                act(W16[4 + fc], cv, AF.Square, reads=[b_hc[fc][t]], writes=[b_w16[4 + fc]])
            b3 = next_ps()
            for fc in range(2):
                mm(psum[b3][:, :], ONES, W16[4 + fc], fc == 0, fc == 1, reads=[b_w16[4 + fc], b_cst],
                   writes=[b_ps[b3]], inc=True)
            rstd_from(psum[b3][:, :], 1.0 / 256, RMS_EPS, 0, 1, [b_ps[b3]])
            for fc in range(2):
                cv = hc[:, fc, 30 + t * TT:30 + (t + 1) * TT]
                v_stt(r1[:, fc, tsl(t)], cv, pc(l, 16 + fc), W32[1], ALU.mult, ALU.mult,
                      reads=[b_hc[fc][t], b_w32[1], b_pcol], writes=[b_r1[fc][t]])

        for t in range(NT):
            rstd_from(SS[:, tsl(t)], 1.0 / 512, RMS_EPS, 0, 1, [b_SS[t]])
            for c in range(4):
                v_stt(r1[:, 4 + c, tsl(t)], r1[:, 4 + c, tsl(t)], pc(l, 20 + c), W32[1], ALU.mult, ALU.mult,
                      reads=[b_r1[4 + c][t], b_w32[1], b_pcol], writes=[b_r1[4 + c][t]])

        phase(6)
        for j in range(4):
            pw, sw = w_get()
            for nn in range(2):
                n = 2 * j + nn
                for t in range(NT):
                    bank = next_ps()
                    for kc in range(8):
                        mm(psum[bank][:, :], wsl[:, sw, kc * 256 + nn * 128:kc * 256 + nn * 128 + 128],
                           r1[:, kc, tsl(t)], kc == 0, kc == 7,
                           reads=[b_ws[sw], b_r1[kc][t]], writes=[b_ps[bank]], inc=(kc == 7))
                    v_tt(xT[:, n, tsl(t)], psum[bank][:, :], xT[:, n, tsl(t)], ALU.add,
                         reads=[b_ps[bank], b_xT[n][t]], writes=[b_xT[n][t]])
            w_release(pw)

        phase(7)
        rmsnorm_to_hT(l, 8)
        for g in range(2):
            for cc in range(11):
                pw, sw = w_get()
                for t in range(NT):
                    bg = next_ps()
                    proj_fm(bg, sw, 0, t)
                    bu = next_ps()
                    proj_fm(bu, sw, 128, t)
                    si = t % 2
                    act(W32[si], psum[bg][:, :], AF.Silu, reads=[b_ps[bg]], writes=[b_w32[si]])
                    v_tt(r1[:, cc, tsl(t)], W32[si], psum[bu][:, :], ALU.mult,
                         reads=[b_w32[si], b_ps[bu]], writes=[b_r1[cc][t]])
                w_release(pw)
            for n in range(8):
                pw, sw = w_get()
                for t in range(NT):
                    bank = next_ps()
                    for kc in range(11):
                        mm(psum[bank][:, :], wsl[:, sw, kc * 128:(kc + 1) * 128], r1[:, kc, tsl(t)],
                           kc == 0, kc == 10, reads=[b_ws[sw], b_r1[kc][t]], writes=[b_ps[bank]], inc=(kc == 10))
                    v_tt(xT[:, n, tsl(t)], psum[bank][:, :], xT[:, n, tsl(t)], ALU.add,
                         reads=[b_ps[bank], b_xT[n][t]], writes=[b_xT[n][t]])
                w_release(pw)

      except _Stop:
        break

    for c in range(8):
        SP.dma(lambda e, c=c: e.dma_start(out=d_out[:, c, :], in_=xT[:, c, :]), ch_o,
               reads=[b_xT[c][t] for t in range(NT)])
    SP.wait_chan(ch_o, ch_o.count)
    assert stop_after is not None or wstate["got"] == len(pieces)

    with nc.Block() as block:
        @block.tensor
        def _(e):
            for f in PE.ops:
                f(e)

        @block.scalar
        def _(e):
            for f in ACT.ops:
                f(e)

        @block.vector
        def _(e):
            for f in DVE.ops:
                f(e)

        @block.gpsimd
        def _(e):
            for f in POOL.ops:
                f(e)

        @block.sync
        def _(e):
            for f in SP.ops:
                f(e)
    return nc


def _kc_layout(w):
    K, N = w.shape
    return np.ascontiguousarray(w.reshape(K // 128, 128, N).transpose(1, 0, 2)).reshape(128, (K // 128) * N)


def _prep_weights(w_in, w_out, w_gate_up, w_down, NL):
    out = np.empty((NL, 128, WTOT), np.float32)
    for l in range(NL):
        parts = []
        wi = w_in[l]
        cols = [wi[:, 0:256], wi[:, 256:512], wi[:, 768:1024], wi[:, 512:768]]
        qk = lambda hp: np.concatenate([wi[:, 1024 + hp * 128:1024 + (hp + 1) * 128],
                                        wi[:, 1536 + hp * 128:1536 + (hp + 1) * 128]], axis=1)
        cols += [qk(0), wi[:, 2048:2304], qk(1), qk(2), wi[:, 2304:2560], qk(3)]
        for c in cols:
            parts.append(_kc_layout(c))
        for j in range(4):
            parts.append(_kc_layout(w_out[l][:, j * 256:(j + 1) * 256]))
        wgu = w_gate_up[l]
        wd = w_down[l]
        for g in range(2):
            for cc in range(11):
                c = g * 11 + cc
                parts.append(_kc_layout(np.concatenate([wgu[:, c * 128:(c + 1) * 128],
                                                        wgu[:, HID + c * 128:HID + (c + 1) * 128]], axis=1)))
            for n in range(8):
                parts.append(_kc_layout(wd[g * 1408:(g + 1) * 1408, n * 128:(n + 1) * 128]))
        out[l] = np.concatenate(parts, axis=1)
    return out


def _prep_params(inp, NL):
    p = np.arange(128)
    pcol = np.empty((128, NL * NPCOL), np.float32)
    pmisc = np.empty((NL, 128, PMISC), np.float32)
    for l in range(NL):
        o = l * NPCOL
        fm = lambda v: v.reshape(-1, 128).T
        pcol[:, o + 0:o + 8] = fm(inp["mix_norm_g"][l])
        pcol[:, o + 8:o + 16] = fm(inp["ffn_norm_g"][l])
        pcol[:, o + 16:o + 24] = fm(inp["out_norm_g"][l])
        cw = inp["conv_w"][l]
        for fc in range(2):
            pcol[:, o + 24 + fc * 31:o + 24 + (fc + 1) * 31] = cw[:, fc * 128:(fc + 1) * 128].T
        pcol[:, o + 86:o + 88] = fm(inp["conv_b"][l])
        pcol[:, o + 88:o + 90] = fm(inp["conv_ln_g"][l])
        pcol[:, o + 90:o + 92] = fm(inp["conv_ln_b"][l])
        pcol[:, o + 92] = inp["q_norm_g"][l][p % 64]
        pcol[:, o + 93] = inp["k_norm_g"][l][p % 64]
        pmisc[l, :, 0:256] = inp["sg_ln_g"][l][None, :]
        pmisc[l, :, 256:512] = inp["sg_ln_b"][l][None, :]
        sgb = inp["sg_b"][l]
        for fc in range(2):
            for rep in range(4):
                base = 512 + fc * 512 + rep * 128
                pmisc[l, 0:64, base:base + 128] = sgb[2 * fc][None, :]
                pmisc[l, 64:128, base:base + 128] = sgb[2 * fc + 1][None, :]
        sgw = inp["sg_w"][l]
        pmisc[l, :, 1536:2048] = sgw.transpose(2, 0, 1).reshape(128, 512)
    return pcol, pmisc


def _consts():
    j = np.arange(128)[:, None]
    s = np.arange(128)[None, :]
    c = np.zeros((128, 8, 128), np.float32)
    c[:, 0, :] = 1.0
    c[:, 1, :] = ((j // 64) == (s // 64)).astype(np.float32)
    c[:, 2, :] = -(j >= s).astype(np.float32)
    c[:, 3, :] = -1.0
    c[:, 4, :] = (j < s).astype(np.float32)
    c[:, 5, :] = (j <= s).astype(np.float32)
    c[:, 6, :] = (j == s).astype(np.float32)
    c[:, 7, :] = -30000.0 * (j >= s).astype(np.float32)
    return c.astype(ml_dtypes.bfloat16)


_PROGRAM_CACHE = {}


def _run(inp, NL):
    x = np.asarray(inp["x"], np.float32)
    B = x.shape[0]
    f = {k: np.asarray(v, np.float32) for k, v in inp.items()}
    wflat = _prep_weights(f["w_in"], f["w_out"], f["w_gate_up"], f["w_down"], NL)
    pcol, pmisc = _prep_params(f, NL)
    cst = _consts()
    if NL not in _PROGRAM_CACHE:
        _PROGRAM_CACHE[NL] = build_program(NL)
    nc = _PROGRAM_CACHE[NL]
    in_maps = []
    for b in range(B):
        xT = np.ascontiguousarray(x[b].T.reshape(8, 128, S).transpose(1, 0, 2))
        m = {"xT": xT, "pcol": pcol, "pmisc": pmisc, "cst": cst}
        for l in range(NL):
            m[f"wflat{l}"] = wflat[l]
        in_maps.append(m)
    res = run_bass_kernel_spmd(nc, in_maps, core_ids=list(range(B)))
    out = np.empty((B, S, D), np.float32)
    for b in range(B):
        oT = np.asarray(res.results[b]["outT"], np.float32)
        out[b] = oT.transpose(1, 0, 2).reshape(D, S).T
    return out


def kernel(**inputs):
    return _run(inputs, L_FULL)
```

```python
import numpy as np
import ml_dtypes
import concourse.bass as bass
import concourse.mybir as mybir
from concourse.bass_utils import run_bass_kernel_spmd

F32 = mybir.dt.float32
BF16 = mybir.dt.bfloat16
AF = mybir.ActivationFunctionType
ALU = mybir.AluOpType

S = 2048
D = 1024
L_FULL = 4
NT = 4
TT = 512
HID = 2816
NSLOT = 5
MAX_INFLIGHT = 1
WSZ = 2048
RMS_EPS = 1e-6
LN_EPS = 1e-5
WTOT = 36 * 2048 + 16 * 1408
NPCOL = 94
PMISC = 2048

SAME_ENGINE_SYNC = True
ATTACH_WAITS = True


class Chan:
    def __init__(self, sem, name):
        self.sem = sem
        self.count = 0
        self.name = name


class Buf:
    __slots__ = ("name", "w", "readers", "excl")

    def __init__(self, name, excl=False):
        self.name = name
        self.w = None
        self.readers = {}
        self.excl = excl


class Eng:
    def __init__(self, name, chan):
        self.name = name
        self.chan = chan
        self.ops = []
        self.waited = {}
        self.pend_r = []
        self.pend_w = []

    def _waits(self, reads, writes):
        need = {}

        def req(tag):
            ch, val = tag
            if ch is self.chan and (self.name == "pe" or not SAME_ENGINE_SYNC):
                return
            if need.get(ch, 0) < val:
                need[ch] = val

        for b in reads:
            if b.w is not None:
                req(b.w)
        for b in writes:
            if b.w is not None:
                req(b.w)
            for ch, val in b.readers.items():
                req((ch, val))
        todo = []
        for ch, val in need.items():
            if self.waited.get(ch, 0) < val:
                self.waited[ch] = val
                todo.append((ch.sem, val))
        self.attach = None
        if todo and self.name in ("act", "dve", "pe") and ATTACH_WAITS:
            self.attach = todo.pop()
        for sem, val in todo:
            self.ops.append(lambda e, s=sem, v=val: e.wait_ge(s, v))

    def op(self, fn, reads=(), writes=(), inc=True):
        if any(b.excl for b in reads):
            writes = list(writes) + [b for b in reads if b.excl]
            reads = [b for b in reads if not b.excl]
        self._waits(reads, writes)
        if not inc:
            a = getattr(self, "attach", None)
            self.attach = None
            self.pend_r.extend(reads)
            self.pend_w.extend(writes)
            if a is not None:
                self.ops.append(lambda e, a=a: fn(e)._wait_ge(a[0], a[1]))
            else:
                self.ops.append(lambda e: fn(e))
            return
        self.chan.count += 1
        tag = (self.chan, self.chan.count)
        att = getattr(self, "attach", None)
        self.attach = None
        if att is not None:
            self.ops.append(lambda e, s=self.chan.sem, a=att: fn(e)._wait_ge(a[0], a[1]).then_inc(s, 1))
        else:
            self.ops.append(lambda e, s=self.chan.sem: fn(e).then_inc(s, 1))
        for b in list(reads) + self.pend_r:
            if b.readers.get(tag[0], 0) < tag[1]:
                b.readers[tag[0]] = tag[1]
        for b in list(writes) + self.pend_w:
            b.w = tag
            b.readers = {}
        self.pend_r = []
        self.pend_w = []

    def dma(self, fn, chan, reads=(), writes=()):
        self._waits(reads, writes)
        chan.count += 16
        tag = (chan, chan.count)
        self.ops.append(lambda e, s=chan.sem: fn(e).then_inc(s, 16))
        for b in reads:
            if b.readers.get(chan, 0) < tag[1]:
                b.readers[chan] = tag[1]
        for b in writes:
            b.w = tag
            b.readers = {}

    def wait_chan(self, chan, val):
        if self.waited.get(chan, 0) < val:
            self.waited[chan] = val
            self.ops.append(lambda e, s=chan.sem, v=val: e.wait_ge(s, v))


class _Stop(Exception):
    pass


def build_program(NL, stop_after=None):
    nc = bass.Bass("TRN2", target_bir_lowering=False)

    d_xT = nc.dram_tensor("xT", [128, 8, S], F32, kind="ExternalInput").ap()
    d_wl = [nc.dram_tensor(f"wflat{l}", [128, WTOT], F32, kind="ExternalInput").ap() for l in range(NL)]
    d_pcol = nc.dram_tensor("pcol", [128, NL * NPCOL], F32, kind="ExternalInput").ap()
    d_pmisc = nc.dram_tensor("pmisc", [NL, 128, PMISC], F32, kind="ExternalInput").ap()
    d_cst = nc.dram_tensor("cst", [128, 8, 128], BF16, kind="ExternalInput").ap()
    d_out = nc.dram_tensor("outT", [128, 8, S], F32, kind="ExternalOutput").ap()

    def mkchan(name):
        return Chan(nc.alloc_semaphore(name), name)

    PE = Eng("pe", mkchan("c_pe"))
    ACT = Eng("act", mkchan("c_act"))
    DVE = Eng("dve", mkchan("c_dve"))
    POOL = Eng("pool", mkchan("c_pool"))
    SP = Eng("sp", mkchan("c_sp"))
    engines = [PE, ACT, DVE, POOL, SP]

    def barrier():
        for e in (PE, ACT, DVE):
            for f in (PE, ACT, DVE):
                if f is not e and f.chan.count > 0:
                    e.wait_chan(f.chan, f.chan.count)

    xT = nc.alloc_sbuf_tensor("xT_sb", [128, 8, S], F32)
    hT = nc.alloc_sbuf_tensor("hT_sb", [128, 8, S], BF16)
    r1 = nc.alloc_sbuf_tensor("r1_sb", [128, 11, S], BF16)
    wsl = nc.alloc_sbuf_tensor("wslots", [128, NSLOT, WSZ], BF16)
    wk32 = nc.alloc_sbuf_tensor("wk32", [128, 4, TT], F32)
    wk16 = nc.alloc_sbuf_tensor("wk16", [128, 8, TT], BF16)
    PHW = 2 * (30 + S)
    ph = nc.alloc_sbuf_tensor("phase", [128, PHW], F32)
    cst = nc.alloc_sbuf_tensor("cst_sb", [128, 8, 128], BF16)
    pcol = nc.alloc_sbuf_tensor("pcol_sb", [128, NL * NPCOL], F32)
    pmisc = nc.alloc_sbuf_tensor("pmisc_sb", [128, PMISC], F32)
    sgw16 = nc.alloc_sbuf_tensor("sgw16", [128, 4, 128], BF16)
    small = nc.alloc_sbuf_tensor("small", [128, 64], F32)
    qgs = nc.alloc_sbuf_tensor("qgs", [128, NL], F32)
    cacc = nc.alloc_sbuf_tensor("cacc", [128, TT], F32)

    hc = ph[:, :].rearrange("p (c t) -> p c t", c=2)
    SS = pmisc[:, 0:S]

    psall = nc.alloc_psum_tensor("psall", [128, 8, TT], F32)
    psum = [psall[:, i, :] for i in range(8)]

    b_xT = [[Buf(f"xT{c}_{t}") for t in range(NT)] for c in range(8)]
    b_hT = [[Buf(f"hT{c}_{t}") for t in range(NT)] for c in range(8)]
    b_r1 = [[Buf(f"r1{c}_{t}") for t in range(NT)] for c in range(11)]
    b_ws = [Buf(f"ws{i}") for i in range(NSLOT)]
    b_w32 = [Buf(f"wk32_{i}") for i in range(4)]
    b_w16 = [Buf(f"wk16_{i}") for i in range(8)]
    b_ps = [Buf(f"ps{i}", excl=True) for i in range(8)]
    b_hc = [[Buf(f"hc{c}_{t}") for t in range(NT)] for c in range(2)]
    b_hcpad = Buf("hcpad")
    b_SS = [Buf(f"SS{t}") for t in range(NT)]
    b_cst = Buf("cst")
    b_pcol = Buf("pcol")
    b_pmisc = Buf("pmisc")
    b_sgw16 = Buf("sgw16")
    b_small = Buf("small")
    b_qgs = Buf("qgs")
    b_cacc = Buf("cacc")

    ch_ws = [mkchan(f"c_ws{i}") for i in range(NSLOT)]
    ch_x = mkchan("c_x")
    ch_c = mkchan("c_cst")
    ch_m = mkchan("c_misc")
    ch_o = mkchan("c_out")

    def tsl(t):
        return slice(t * TT, (t + 1) * TT)

    W32 = [wk32[:, i, :] for i in range(4)]
    W16 = [wk16[:, i, :] for i in range(8)]
    ONES = cst[:, 0, :]
    BD64 = cst[:, 1, :]
    NEGU = cst[:, 2, :]
    NEGONES = cst[:, 3, :]
    MASK_LT = cst[:, 4, :]
    MASK_LE = cst[:, 5, :]
    IDENT = cst[:, 6, :]
    NEGMASK = cst[:, 7, :]

    ps_rr = [0]

    def next_ps():
        i = ps_rr[0]
        ps_rr[0] = (i + 1) % 6
        return i

    def next_ps_pair():
        i = ps_rr[0]
        if i % 2:
            i = (i + 1) % 6
        ps_rr[0] = (i + 2) % 6
        return i

    def mm(out, lhsT, rhs, start, stop, reads, writes, inc, skip=False):
        PE.op(lambda e: e.matmul(out, lhsT=lhsT, rhs=rhs, start=start, stop=stop, skip_group_check=skip),
              reads=reads, writes=writes, inc=inc)

    def act(out, in_, func, reads, writes, bias=None, scale=None):
        kw = {}
        if bias is not None:
            kw["bias"] = bias
        if scale is not None:
            kw["scale"] = scale
        ACT.op(lambda e: e.activation(out=out, in_=in_, func=func, **kw), reads=reads, writes=writes)

    def v_tt(out, in0, in1, op, reads, writes):
        DVE.op(lambda e: e.tensor_tensor(out=out, in0=in0, in1=in1, op=op), reads=reads, writes=writes)

    def v_ts(out, in0, s1, s2, op0, op1, reads, writes):
        if s2 is None:
            DVE.op(lambda e: e.tensor_scalar(out=out, in0=in0, scalar1=s1, scalar2=None, op0=op0),
                   reads=reads, writes=writes)
        else:
            DVE.op(lambda e: e.tensor_scalar(out=out, in0=in0, scalar1=s1, scalar2=s2, op0=op0, op1=op1),
                   reads=reads, writes=writes)

    def v_stt(out, in0, scalar, in1, op0, op1, reads, writes):
        DVE.op(lambda e: e.scalar_tensor_tensor(out=out, in0=in0, scalar=scalar, in1=in1, op0=op0, op1=op1),
               reads=reads, writes=writes)

    def v_copy(out, in_, reads, writes):
        DVE.op(lambda e: e.tensor_copy(out=out, in_=in_), reads=reads, writes=writes)

    def rstd_from(ps_ap, scale, eps, tmp_i, out_i, rd_bufs):
        act(W32[tmp_i], ps_ap, AF.Ln, reads=rd_bufs, writes=[b_w32[tmp_i]], bias=eps, scale=scale)
        act(W32[out_i], W32[tmp_i], AF.Exp, reads=[b_w32[tmp_i]], writes=[b_w32[out_i]], scale=-0.5)

    pieces = []
    for l in range(NL):
        off = 0
        for _ in range(14):
            pieces.append((l, off, 2048)); off += 2048
        for g in range(2):
            for _ in range(11):
                pieces.append((l, off, 2048)); off += 2048
            for _ in range(8):
                pieces.append((l, off, 1408)); off += 1408
        assert off == WTOT
    wstate = {"issued": 0, "got": 0, "free": list(range(NSLOT)), "slot_of": {}}

    def w_issue():
        while wstate["free"] and wstate["issued"] < len(pieces):
            i = wstate["issued"]
            sl = wstate["free"].pop(0)
            l, off, sz = pieces[i]
            hist = wstate.setdefault("hist", [])
            if len(hist) >= MAX_INFLIGHT:
                pch, pval = hist[-MAX_INFLIGHT]
                POOL.wait_chan(pch, pval)
            hist.append((ch_ws[sl], ch_ws[sl].count + 16))
            POOL.dma(lambda e, sl=sl, l=l, off=off, sz=sz: e.dma_start(out=wsl[:, sl, 0:sz], in_=d_wl[l][:, off:off + sz]),
                     ch_ws[sl], reads=(), writes=[b_ws[sl]])
            wstate["slot_of"][i] = sl
            wstate["issued"] += 1

    def w_get():
        i = wstate["got"]
        assert i in wstate["slot_of"], "weight piece not issued (too many live pieces)"
        wstate["got"] += 1
        return i, wstate["slot_of"][i]

    def w_release(i):
        sl = wstate["slot_of"].pop(i)
        wstate["free"].append(sl)
        w_issue()

    SP.dma(lambda e: e.dma_start(out=cst[:], in_=d_cst), ch_c, writes=[b_cst])
    SP.dma(lambda e: e.dma_start(out=pcol[:], in_=d_pcol), ch_c, writes=[b_pcol])
    for c in range(8):
        SP.dma(lambda e, c=c: e.dma_start(out=xT[:, c, :], in_=d_xT[:, c, :]), ch_x,
               writes=[b_xT[c][t] for t in range(NT)])
    for c in range(8):
        for t in range(NT):
            b_xT[c][t].w = (ch_x, ch_x.count)
    b_cst.w = (ch_c, ch_c.count)
    b_pcol.w = (ch_c, ch_c.count)
    w_issue()

    for l in range(NL):
        v_ts(qgs[:, l:l + 1], pcol[:, l * NPCOL + 92:l * NPCOL + 93], 0.125, None, ALU.mult, None,
             reads=[b_pcol], writes=[b_qgs])

    def pc(l, j):
        return pcol[:, l * NPCOL + j:l * NPCOL + j + 1]

    def rmsnorm_to_hT(l, goff):
        for t in range(NT):
            bank = next_ps()
            for c in range(8):
                si = c % 4
                act(W16[si], xT[:, c, tsl(t)], AF.Square, reads=[b_xT[c][t]], writes=[b_w16[si]])
                mm(psum[bank][:, :], ONES, W16[si], c == 0, c == 7,
                   reads=[b_w16[si], b_cst], writes=[b_ps[bank]], inc=True)
            rstd_from(psum[bank][:, :], 1.0 / D, RMS_EPS, 0, 1, [b_ps[bank]])
            for c in range(8):
                v_stt(hT[:, c, tsl(t)], xT[:, c, tsl(t)], pc(l, goff + c), W32[1], ALU.mult, ALU.mult,
                      reads=[b_xT[c][t], b_w32[1], b_pcol], writes=[b_hT[c][t]])

    def proj_fm(bank, slot, col0, t, last_inc=True):
        for kc in range(8):
            mm(psum[bank][:, :], wsl[:, slot, kc * 256 + col0: kc * 256 + col0 + 128], hT[:, kc, tsl(t)],
               kc == 0, kc == 7, reads=[b_ws[slot], b_hT[kc][t]], writes=[b_ps[bank]],
               inc=(kc == 7))

    def phase(k):
        if stop_after is not None and k >= stop_after:
            raise _Stop()

    for l in range(NL):
      try:
        SP.dma(lambda e, l=l: e.dma_start(out=pmisc[:], in_=d_pmisc[l]), ch_m, writes=[b_pmisc] + b_SS)
        for h in range(4):
            v_tt(sgw16[:, h, :], pmisc[:, 1536 + h * 128:1536 + (h + 1) * 128], MASK_LE, ALU.mult,
                 reads=[b_pmisc, b_cst], writes=[b_sgw16])
        DVE.op(lambda e: e.memset(hc[:, :, 0:30], 0.0), writes=[b_hcpad])

        phase(0)
        rmsnorm_to_hT(l, 0)
        phase(1)

        pv, sv = w_get()
        pg, sg_ = w_get()
        for fc in range(2):
            for t in range(NT):
                bv = next_ps()
                proj_fm(bv, sv, fc * 128, t)
                bg = next_ps()
                proj_fm(bg, sg_, fc * 128, t)
                act(W32[0], psum[bg][:, :], AF.Sigmoid, reads=[b_ps[bg]], writes=[b_w32[0]])
                v_tt(hc[:, fc, 30 + t * TT:30 + (t + 1) * TT], psum[bv][:, :], W32[0], ALU.mult,
                     reads=[b_ps[bv], b_w32[0]], writes=[b_hc[fc][t]])
        w_release(pv)
        w_release(pg)
        phase(2)
        fillers = []
        for fc in range(2):
            for t in range(NT - 1, -1, -1):
                rd = [b_hc[fc][t], b_hcpad, b_pcol] + ([b_hc[fc][t - 1]] if t > 0 else [])
                cw0 = 24 + fc * 31
                fillers.append(lambda fc=fc, t=t, rd=rd, cw0=cw0: v_ts(
                    cacc[:, :], hc[:, fc, t * TT:t * TT + TT], pc(l, cw0), pc(l, 86 + fc), ALU.mult, ALU.add,
                    reads=rd, writes=[b_cacc]))
                for k in range(1, 30):
                    fillers.append(lambda fc=fc, t=t, rd=rd, cw0=cw0, k=k: v_stt(
                        cacc[:, :], hc[:, fc, t * TT + k:t * TT + k + TT], pc(l, cw0 + k), cacc[:, :], ALU.mult, ALU.add,
                        reads=rd + [b_cacc], writes=[b_cacc]))
                fillers.append(lambda fc=fc, t=t, rd=rd, cw0=cw0: v_stt(
                    hc[:, fc, 30 + t * TT:30 + t * TT + TT], hc[:, fc, 30 + t * TT:30 + t * TT + TT],
                    pc(l, cw0 + 30), cacc[:, :], ALU.mult, ALU.add,
                    reads=rd + [b_cacc], writes=[b_hc[fc][t]]))

        def run_fillers(n):
            for _ in range(n):
                if fillers:
                    fillers.pop(0)()

        phase(4)
        pvv, svv = w_get()
        pu, su = w_get()
        for t in range(NT):
            for sub in range(4):
                ck = 4 * t + sub
                bank = next_ps()
                for kc in range(8):
                    mm(psum[bank][:, 0:256], hT[:, kc, ck * 128:(ck + 1) * 128], wsl[:, svv, kc * 256:(kc + 1) * 256],
                       kc == 0, kc == 7, reads=[b_ws[svv], b_hT[kc][t]], writes=[b_ps[bank]], inc=(kc == 7))
                wi = sub // 2
                vg = wk32[:, wi, (sub % 2) * 256:(sub % 2) * 256 + 256]
                act(vg, psum[bank][:, 0:256], AF.Gelu, reads=[b_ps[bank]], writes=[b_w32[wi]])
                DVE.op(lambda e, vg=vg, sub=sub: e.bn_stats(out=small[:, sub * 6:sub * 6 + 6], in_=vg),
                       reads=[b_w32[wi]], writes=[b_small])
                DVE.op(lambda e, sub=sub: e.bn_aggr(out=small[:, 24 + sub * 2:24 + sub * 2 + 2],
                                                    in_=small[:, sub * 6:sub * 6 + 6]),
                       reads=[b_small], writes=[b_small])
            var4 = small[:, 24:32].rearrange("p (a b) -> p a b", b=2)[:, :, 1]
            act(small[:, 32:36], var4, AF.Ln, reads=[b_small], writes=[b_small], bias=LN_EPS)
            act(small[:, 36:40], small[:, 32:36], AF.Exp, reads=[b_small], writes=[b_small], scale=-0.5)
            for sub in range(4):
                wi = sub // 2
                vg = wk32[:, wi, (sub % 2) * 256:(sub % 2) * 256 + 256]
                v_ts(vg, vg, small[:, 24 + sub * 2:25 + sub * 2], small[:, 36 + sub:37 + sub], ALU.subtract, ALU.mult,
                     reads=[b_w32[wi], b_small], writes=[b_w32[wi]])
                v_tt(vg, vg, pmisc[:, 0:256], ALU.mult, reads=[b_w32[wi], b_pmisc], writes=[b_w32[wi]])
                vh = wk16[:, wi, (sub % 2) * 256:(sub % 2) * 256 + 256]
                v_tt(vh, vg, pmisc[:, 256:512], ALU.add, reads=[b_w32[wi], b_pmisc], writes=[b_w16[wi]])
            for fc in range(2):
                bank = next_ps()
                proj_fm(bank, su, fc * 128, t)
                act(W32[fc], psum[bank][:, :], AF.Gelu, reads=[b_ps[bank]], writes=[b_w32[fc]])
            for fc in range(2):
                bank = next_ps()
                for sub in range(4):
                    for hh in range(2):
                        h = 2 * fc + hh
                        wi = sub // 2
                        vh = wk16[:, wi, (sub % 2) * 256 + h * 64:(sub % 2) * 256 + h * 64 + 64]
                        mm(psum[bank][hh * 64:(hh + 1) * 64, sub * 128:(sub + 1) * 128], vh, sgw16[:, h, :],
                           True, True, reads=[b_w16[wi], b_sgw16], writes=[b_ps[bank]],
                           inc=(sub == 3 and hh == 1))
                v_tt(W32[2 + fc], psum[bank][:, :], pmisc[:, 512 + fc * 512:512 + (fc + 1) * 512], ALU.add,
                     reads=[b_ps[bank], b_pmisc], writes=[b_w32[2 + fc]])
                v_tt(W32[2 + fc], W32[2 + fc], W32[fc], ALU.mult, reads=[b_w32[2 + fc], b_w32[fc]],
                     writes=[b_w32[2 + fc]])
                act(W16[2 + fc], W32[2 + fc], AF.Square, reads=[b_w32[2 + fc]], writes=[b_w16[2 + fc]])
            b3 = next_ps()
            for fc in range(2):
                mm(psum[b3][:, :], ONES, W16[2 + fc], fc == 0, fc == 1, reads=[b_w16[2 + fc], b_cst],
                   writes=[b_ps[b3]], inc=True)
            rstd_from(psum[b3][:, :], 1.0 / 256, RMS_EPS, 0, 1, [b_ps[b3]])
            for fc in range(2):
                v_stt(r1[:, 2 + fc, tsl(t)], W32[2 + fc], pc(l, 18 + fc), W32[1], ALU.mult, ALU.mult,
                      reads=[b_w32[2 + fc], b_w32[1], b_pcol], writes=[b_r1[2 + fc][t]])
        w_release(pvv)
        w_release(pu)

        phase(5)
        qT = r1[:, 8, :]
        kT = r1[:, 9, :]
        v16 = r1[:, 10, :].rearrange("p (c d) -> p c d", d=128)
        pieces_v = {}
        for hp in range(4):
            pqk, sqk = w_get()
            if hp % 2 == 0:
                pieces_v["cur"] = w_get()
            pvp, svp = pieces_v["cur"]
            for t in range(NT):
                for which in range(2):
                    bank = next_ps()
                    proj_fm(bank, sqk, which * 128, t)
                    v_copy(W32[0], psum[bank][:, :], reads=[b_ps[bank]], writes=[b_w32[0]])
                    act(W16[0], psum[bank][:, :], AF.Square, reads=[b_ps[bank]], writes=[b_w16[0]])
                    b2 = next_ps()
                    mm(psum[b2][:, :], BD64, W16[0], True, True, reads=[b_w16[0], b_cst], writes=[b_ps[b2]], inc=True)
                    rstd_from(psum[b2][:, :], 1.0 / 64, RMS_EPS, 1, 2, [b_ps[b2]])
                    gcol = qgs[:, l:l + 1] if which == 0 else pc(l, 93)
                    dst = (qT if which == 0 else kT)[:, tsl(t)]
                    v_stt(dst, W32[0], gcol, W32[2], ALU.mult, ALU.mult,
                          reads=[b_w32[0], b_w32[2], b_qgs, b_pcol], writes=[b_r1[8 + which][t]])
            phase(5.2)
            for t in range(NT):
                bank = next_ps()
                for sub in range(4):
                    ck = 4 * t + sub
                    for kc in range(8):
                        c0 = kc * 256 + (hp % 2) * 128
                        mm(psum[bank][:, sub * 128:(sub + 1) * 128], hT[:, kc, ck * 128:(ck + 1) * 128],
                           wsl[:, svp, c0:c0 + 128], kc == 0, kc == 7,
                           reads=[b_ws[svp], b_hT[kc][t]], writes=[b_ps[bank]], inc=(kc == 7 and sub == 3))
                v_copy(r1[:, 10, tsl(t)], psum[bank][:, :], reads=[b_ps[bank]], writes=[b_r1[10][t]])
            w_release(pqk)
            if hp % 2 == 1:
                w_release(pvp)
            phase(5.4)
            iters = [(qt, kb) for qt in range(NT) for kb in range(4 * qt + 3, -1, -1)]
            ctxs = {}

            def setup(i):
                qt, kb = iters[i]
                ob = 6 + (qt % 2)
                diag = kb >= 4 * qt
                c0 = (kb - 4 * qt) * 128 if diag else 0
                kt = kb // 4
                cs = slice(c0, TT)
                first = (kb == 4 * qt + 3)
                cc0 = c0 + 128 if diag else 0
                has_carry = (not first) and cc0 < TT
                par = kb % 2
                zb = next_ps_pair()
                ctxs[i] = (qt, kb, ob, diag, c0, kt, cs, first, cc0, has_carry, par, zb)

            def stage_a1(i):
                qt, kb, ob, diag, c0, kt, cs, first, cc0, has_carry, par, zb = ctxs[i]
                bz = [b_ps[zb], b_ps[zb + 1]]
                bE = [b_w32[0], b_w32[1]]
                for hh in range(2):
                    hs = slice(hh * 64, hh * 64 + 64)
                    mm(psall[:, zb + hh, cs], kT[hs, kb * 128:(kb + 1) * 128], qT[hs, qt * TT + c0:(qt + 1) * TT],
                       True, True, reads=[b_r1[9][kt], b_r1[8][qt]], writes=[b_ps[zb + hh]], inc=not diag)
                    if diag:
                        mm(psall[:, zb + hh, c0:c0 + 128], IDENT, NEGMASK, False, True,
                           reads=[b_cst], writes=[b_ps[zb + hh]], inc=True, skip=True)
                act(wk32[:, 0:2, cs], psall[:, zb:zb + 2, cs], AF.Exp, reads=bz, writes=bE)

            def stage_a2(i):
                qt, kb, ob, diag, c0, kt, cs, first, cc0, has_carry, par, zb = ctxs[i]
                L0 = 2 * par
                bE = [b_w32[0], b_w32[1]]
                bL = [b_w16[L0], b_w16[L0 + 1]]
                act(wk16[:, L0:L0 + 2, cs], wk32[:, 0:2, cs], AF.Ln, reads=bE, writes=bL, bias=1.0)

            def stage_b1(i):
                qt, kb, ob, diag, c0, kt, cs, first, cc0, has_carry, par, zb = ctxs[i]
                L0 = 2 * par
                bz = [b_ps[zb], b_ps[zb + 1]]
                bL = [b_w16[L0], b_w16[L0 + 1]]
                bA = [b_w16[4], b_w16[5]]
                bR = [b_w32[2], b_w32[3]]
                bR16 = [b_w16[6], b_w16[7]]
                for hh in range(2):
                    mm(psall[:, zb + hh, cs], NEGU, wk16[:, L0 + hh, cs], False, not has_carry,
                       reads=[b_w16[L0 + hh], b_cst], writes=[b_ps[zb + hh]], inc=not has_carry, skip=True)
                    if has_carry:
                        mm(psall[:, zb + hh, cc0:TT], NEGONES, wk16[:, 6 + hh, cc0:TT], False, True,
                           reads=[b_w16[6 + hh], b_cst], writes=[b_ps[zb + hh]], inc=True, skip=True)
                act(wk16[:, 4:6, cs], psall[:, zb:zb + 2, cs], AF.Exp, reads=bz, writes=bA)

            def stage_b2(i):
                qt, kb, ob, diag, c0, kt, cs, first, cc0, has_carry, par, zb = ctxs[i]
                L0 = 2 * par
                bL = [b_w16[L0], b_w16[L0 + 1]]
                bR = [b_w32[2], b_w32[3]]
                bR16 = [b_w16[6], b_w16[7]]
                for hh in range(2):
                    hs = slice(hh * 64, hh * 64 + 64)
                    mm(psum[ob][hs, cs], v16[:, kb, hs], wk16[:, 4 + hh, cs], first, kb == 0,
                       reads=[b_r1[10][kt], b_w16[4 + hh]], writes=[b_ps[ob]], inc=True, skip=True)
                if kb > 0:
                    if diag:
                        v_copy(wk32[:, 2:4, c0:c0 + 128], wk16[:, L0:L0 + 2, c0:c0 + 128], reads=bL, writes=bR)
                        if c0 + 128 < TT:
                            v_tt(wk32[:, 2:4, c0 + 128:TT], wk32[:, 2:4, c0 + 128:TT], wk16[:, L0:L0 + 2, c0 + 128:TT],
                                 ALU.add, reads=bR + bL, writes=bR)
                    else:
                        v_tt(wk32[:, 2:4, :], wk32[:, 2:4, :], wk16[:, L0:L0 + 2, :], ALU.add,
                             reads=bR + bL, writes=bR)
                    v_copy(wk16[:, 6:8, cs], wk32[:, 2:4, cs], reads=bR, writes=bR16)
                if kb == 0:
                    phase(5.5)
                    v_copy(r1[:, 4 + hp, tsl(qt)], psum[ob][:, :], reads=[b_ps[ob]], writes=[b_r1[4 + hp][qt]])
                    act(W16[0], psum[ob][:, :], AF.Square, reads=[b_ps[ob]], writes=[b_w16[0]])
                    b2 = next_ps()
                    mm(psum[b2][:, :], ONES, W16[0], True, True, reads=[b_w16[0], b_cst], writes=[b_ps[b2]], inc=True)
                    if hp == 0:
                        v_copy(SS[:, tsl(qt)], psum[b2][:, :], reads=[b_ps[b2]], writes=[b_SS[qt], b_pmisc])
                    else:
                        v_tt(SS[:, tsl(qt)], SS[:, tsl(qt)], psum[b2][:, :], ALU.add,
                             reads=[b_ps[b2], b_SS[qt]], writes=[b_SS[qt], b_pmisc])

            setup(0)
            stage_a1(0)
            stage_a2(0)
            for i in range(len(iters)):
                nxt = i + 1 < len(iters)
                if nxt:
                    setup(i + 1)
                    stage_a1(i + 1)
                stage_b1(i)
                if nxt:
                    stage_a2(i + 1)
                stage_b2(i)
                run_fillers(2)
        run_fillers(len(fillers))
        phase(3)
        for t in range(NT):
            b1 = next_ps()
            b2 = next_ps()
            for fc in range(2):
                cv = hc[:, fc, 30 + t * TT:30 + (t + 1) * TT]
                act(W16[fc], cv, AF.Copy, reads=[b_hc[fc][t]], writes=[b_w16[fc]])
                act(W16[2 + fc], cv, AF.Square, reads=[b_hc[fc][t]], writes=[b_w16[2 + fc]])
            for fc in range(2):
                mm(psum[b1][:, :], ONES, W16[fc], fc == 0, fc == 1, reads=[b_w16[fc], b_cst],
                   writes=[b_ps[b1]], inc=True)
            for fc in range(2):
                mm(psum[b2][:, :], ONES, W16[2 + fc], fc == 0, fc == 1, reads=[b_w16[2 + fc], b_cst],
                   writes=[b_ps[b2]], inc=True)
            v_ts(W32[0], psum[b1][:, :], 1.0 / 256, None, ALU.mult, None, reads=[b_ps[b1]], writes=[b_w32[0]])
            v_tt(W32[1], W32[0], W32[0], ALU.mult, reads=[b_w32[0]], writes=[b_w32[1]])
            v_stt(W32[1], psum[b2][:, :], 1.0 / 256, W32[1], ALU.mult, ALU.subtract,
                  reads=[b_ps[b2], b_w32[1]], writes=[b_w32[1]])
            rstd_from(W32[1], 1.0, LN_EPS, 2, 3, [b_w32[1]])
            for fc in range(2):
                cv = hc[:, fc, 30 + t * TT:30 + (t + 1) * TT]
                v_tt(cv, cv, W32[0], ALU.subtract, reads=[b_hc[fc][t], b_w32[0]], writes=[b_hc[fc][t]])
                v_tt(cv, cv, W32[3], ALU.mult, reads=[b_hc[fc][t], b_w32[3]], writes=[b_hc[fc][t]])
                act(cv, cv, AF.Silu, reads=[b_hc[fc][t], b_pcol], writes=[b_hc[fc][t]],
                    bias=pc(l, 90 + fc), scale=pc(l, 88 + fc))
                act(W16[4 + fc], cv, AF.Square, reads=[b_hc[fc][t]], writes=[b_w16[4 + fc]])
            b3 = next_ps()
            for fc in range(2):
                mm(psum[b3][:, :], ONES, W16[4 + fc], fc == 0, fc == 1, reads=[b_w16[4 + fc], b_cst],
                   writes=[b_ps[b3]], inc=True)
            rstd_from(psum[b3][:, :], 1.0 / 256, RMS_EPS, 0, 1, [b_ps[b3]])
            for fc in range(2):
                cv = hc[:, fc, 30 + t * TT:30 + (t + 1) * TT]
                v_stt(r1[:, fc, tsl(t)], cv, pc(l, 16 + fc), W32[1], ALU.mult, ALU.mult,
                      reads=[b_hc[fc][t], b_w32[1], b_pcol], writes=[b_r1[fc][t]])

        for t in range(NT):
            rstd_from(SS[:, tsl(t)], 1.0 / 512, RMS_EPS, 0, 1, [b_SS[t]])
            for c in range(4):
                v_stt(r1[:, 4 + c, tsl(t)], r1[:, 4 + c, tsl(t)], pc(l, 20 + c), W32[1], ALU.mult, ALU.mult,
                      reads=[b_r1[4 + c][t], b_w32[1], b_pcol], writes=[b_r1[4 + c][t]])

        phase(6)
        for j in range(4):
            pw, sw = w_get()
            for nn in range(2):
                n = 2 * j + nn
                for t in range(NT):
                    bank = next_ps()
                    for kc in range(8):
                        mm(psum[bank][:, :], wsl[:, sw, kc * 256 + nn * 128:kc * 256 + nn * 128 + 128],
                           r1[:, kc, tsl(t)], kc == 0, kc == 7,
                           reads=[b_ws[sw], b_r1[kc][t]], writes=[b_ps[bank]], inc=(kc == 7))
                    v_tt(xT[:, n, tsl(t)], psum[bank][:, :], xT[:, n, tsl(t)], ALU.add,
                         reads=[b_ps[bank], b_xT[n][t]], writes=[b_xT[n][t]])
            w_release(pw)

        phase(7)
        rmsnorm_to_hT(l, 8)
        for g in range(2):
            for cc in range(11):
                pw, sw = w_get()
                for t in range(NT):
                    bg = next_ps()
                    proj_fm(bg, sw, 0, t)
                    bu = next_ps()
                    proj_fm(bu, sw, 128, t)
                    si = t % 2
                    act(W32[si], psum[bg][:, :], AF.Silu, reads=[b_ps[bg]], writes=[b_w32[si]])
                    v_tt(r1[:, cc, tsl(t)], W32[si], psum[bu][:, :], ALU.mult,
                         reads=[b_w32[si], b_ps[bu]], writes=[b_r1[cc][t]])
                w_release(pw)
            for n in range(8):
                pw, sw = w_get()
                for t in range(NT):
                    bank = next_ps()
                    for kc in range(11):
                        mm(psum[bank][:, :], wsl[:, sw, kc * 128:(kc + 1) * 128], r1[:, kc, tsl(t)],
                           kc == 0, kc == 10, reads=[b_ws[sw], b_r1[kc][t]], writes=[b_ps[bank]], inc=(kc == 10))
                    v_tt(xT[:, n, tsl(t)], psum[bank][:, :], xT[:, n, tsl(t)], ALU.add,
                         reads=[b_ps[bank], b_xT[n][t]], writes=[b_xT[n][t]])
                w_release(pw)

      except _Stop:
        break

    for c in range(8):
        SP.dma(lambda e, c=c: e.dma_start(out=d_out[:, c, :], in_=xT[:, c, :]), ch_o,
               reads=[b_xT[c][t] for t in range(NT)])
    SP.wait_chan(ch_o, ch_o.count)
    assert stop_after is not None or wstate["got"] == len(pieces)

    with nc.Block() as block:
        @block.tensor
        def _(e):
            for f in PE.ops:
                f(e)

        @block.scalar
        def _(e):
            for f in ACT.ops:
                f(e)

        @block.vector
        def _(e):
            for f in DVE.ops:
                f(e)

        @block.gpsimd
        def _(e):
            for f in POOL.ops:
                f(e)

        @block.sync
        def _(e):
            for f in SP.ops:
                f(e)
    return nc


def _kc_layout(w):
    K, N = w.shape
    return np.ascontiguousarray(w.reshape(K // 128, 128, N).transpose(1, 0, 2)).reshape(128, (K // 128) * N)


def _prep_weights(w_in, w_out, w_gate_up, w_down, NL):
    out = np.empty((NL, 128, WTOT), np.float32)
    for l in range(NL):
        parts = []
        wi = w_in[l]
        cols = [wi[:, 0:256], wi[:, 256:512], wi[:, 768:1024], wi[:, 512:768]]
        qk = lambda hp: np.concatenate([wi[:, 1024 + hp * 128:1024 + (hp + 1) * 128],
                                        wi[:, 1536 + hp * 128:1536 + (hp + 1) * 128]], axis=1)
        cols += [qk(0), wi[:, 2048:2304], qk(1), qk(2), wi[:, 2304:2560], qk(3)]
        for c in cols:
            parts.append(_kc_layout(c))
        for j in range(4):
            parts.append(_kc_layout(w_out[l][:, j * 256:(j + 1) * 256]))
        wgu = w_gate_up[l]
        wd = w_down[l]
        for g in range(2):
            for cc in range(11):
                c = g * 11 + cc
                parts.append(_kc_layout(np.concatenate([wgu[:, c * 128:(c + 1) * 128],
                                                        wgu[:, HID + c * 128:HID + (c + 1) * 128]], axis=1)))
            for n in range(8):
                parts.append(_kc_layout(wd[g * 1408:(g + 1) * 1408, n * 128:(n + 1) * 128]))
        out[l] = np.concatenate(parts, axis=1)
    return out


def _prep_params(inp, NL):
    p = np.arange(128)
    pcol = np.empty((128, NL * NPCOL), np.float32)
    pmisc = np.empty((NL, 128, PMISC), np.float32)
    for l in range(NL):
        o = l * NPCOL
        fm = lambda v: v.reshape(-1, 128).T
        pcol[:, o + 0:o + 8] = fm(inp["mix_norm_g"][l])
        pcol[:, o + 8:o + 16] = fm(inp["ffn_norm_g"][l])
        pcol[:, o + 16:o + 24] = fm(inp["out_norm_g"][l])
        cw = inp["conv_w"][l]
        for fc in range(2):
            pcol[:, o + 24 + fc * 31:o + 24 + (fc + 1) * 31] = cw[:, fc * 128:(fc + 1) * 128].T
        pcol[:, o + 86:o + 88] = fm(inp["conv_b"][l])
        pcol[:, o + 88:o + 90] = fm(inp["conv_ln_g"][l])
        pcol[:, o + 90:o + 92] = fm(inp["conv_ln_b"][l])
        pcol[:, o + 92] = inp["q_norm_g"][l][p % 64]
        pcol[:, o + 93] = inp["k_norm_g"][l][p % 64]
        pmisc[l, :, 0:256] = inp["sg_ln_g"][l][None, :]
        pmisc[l, :, 256:512] = inp["sg_ln_b"][l][None, :]
        sgb = inp["sg_b"][l]
        for fc in range(2):
            for rep in range(4):
                base = 512 + fc * 512 + rep * 128
                pmisc[l, 0:64, base:base + 128] = sgb[2 * fc][None, :]
                pmisc[l, 64:128, base:base + 128] = sgb[2 * fc + 1][None, :]
        sgw = inp["sg_w"][l]
        pmisc[l, :, 1536:2048] = sgw.transpose(2, 0, 1).reshape(128, 512)
    return pcol, pmisc


def _consts():
    j = np.arange(128)[:, None]
    s = np.arange(128)[None, :]
    c = np.zeros((128, 8, 128), np.float32)
    c[:, 0, :] = 1.0
    c[:, 1, :] = ((j // 64) == (s // 64)).astype(np.float32)
    c[:, 2, :] = -(j >= s).astype(np.float32)
    c[:, 3, :] = -1.0
    c[:, 4, :] = (j < s).astype(np.float32)
    c[:, 5, :] = (j <= s).astype(np.float32)
    c[:, 6, :] = (j == s).astype(np.float32)
    c[:, 7, :] = -30000.0 * (j >= s).astype(np.float32)
    return c.astype(ml_dtypes.bfloat16)


_PROGRAM_CACHE = {}


def _run(inp, NL):
    x = np.asarray(inp["x"], np.float32)
    B = x.shape[0]
    f = {k: np.asarray(v, np.float32) for k, v in inp.items()}
    wflat = _prep_weights(f["w_in"], f["w_out"], f["w_gate_up"], f["w_down"], NL)
    pcol, pmisc = _prep_params(f, NL)
    cst = _consts()
    if NL not in _PROGRAM_CACHE:
        _PROGRAM_CACHE[NL] = build_program(NL)
    nc = _PROGRAM_CACHE[NL]
    in_maps = []
    for b in range(B):
        xT = np.ascontiguousarray(x[b].T.reshape(8, 128, S).transpose(1, 0, 2))
        m = {"xT": xT, "pcol": pcol, "pmisc": pmisc, "cst": cst}
        for l in range(NL):
            m[f"wflat{l}"] = wflat[l]
        in_maps.append(m)
    res = run_bass_kernel_spmd(nc, in_maps, core_ids=list(range(B)))
    out = np.empty((B, S, D), np.float32)
    for b in range(B):
        oT = np.asarray(res.results[b]["outT"], np.float32)
        out[b] = oT.transpose(1, 0, 2).reshape(D, S).T
    return out


def kernel(**inputs):
    return _run(inputs, L_FULL)
```
